# Optimizing a Trainium2 kernel written in Bass

```python
import jax, jax.numpy as jnp
from jax import lax
import numpy as np

D_MODEL = 1024
BATCH = 16
SEQ = 2048
DEPTH = 1

CHUNK = 64
Q_BLOCK = 128
EPS = 1e-6
D_FF = 2816

FOX_HEADS = 8
FOX_HEAD_DIM = 64
FOX_WIDTH = FOX_HEADS * FOX_HEAD_DIM

GLA_HEADS = 4
GLA_DK = 64
GLA_DV = 128
GLA_KEY_WIDTH = GLA_HEADS * GLA_DK
GLA_VAL_WIDTH = GLA_HEADS * GLA_DV
GLA_GATE_RANK = 16
GLA_GATE_TEMP = 16.0

MIX_WIDTH = FOX_WIDTH + GLA_VAL_WIDTH
IN_SPLITS = (FOX_WIDTH, FOX_WIDTH, FOX_WIDTH, FOX_HEADS,
             GLA_KEY_WIDTH, GLA_KEY_WIDTH, GLA_VAL_WIDTH, GLA_GATE_RANK, GLA_VAL_WIDTH)
IN_WIDTH = 3 * FOX_WIDTH + FOX_HEADS + 2 * GLA_KEY_WIDTH + GLA_VAL_WIDTH + GLA_GATE_RANK + GLA_VAL_WIDTH

kernel_name = "hymba_fox_gla_macaron_block"


def rms_norm(x, g):
    xf = x.astype(jnp.float32)
    y = xf * lax.rsqrt(jnp.mean(xf * xf, axis=-1, keepdims=True) + EPS)
    return (y * g.astype(jnp.float32)).astype(x.dtype)


def swiglu(h, w_gate, w_up, w_down):
    return (jax.nn.silu(h @ w_gate) * (h @ w_up)) @ w_down


def fox_attention(q, k, v, log_f):
    S, Dh = q.shape[1], q.shape[-1]
    F = jnp.transpose(jnp.cumsum(log_f, axis=1), (0, 2, 1))
    scale = Dh ** -0.5
    outs = []
    for blk in range(S // Q_BLOCK):
        q0, q1 = blk * Q_BLOCK, (blk + 1) * Q_BLOCK
        qb, kb, vb = q[:, q0:q1], k[:, :q1], v[:, :q1]
        s = jnp.einsum('bqhd,bkhd->bhqk', qb, kb).astype(jnp.float32) * scale
        s = s + F[:, :, q0:q1, None] - F[:, :, None, :q1]
        causal = jnp.arange(q1)[None, :] <= jnp.arange(q0, q1)[:, None]
        s = jnp.where(causal, s, -jnp.inf)
        p = jax.nn.softmax(s, axis=-1).astype(vb.dtype)
        outs.append(jnp.einsum('bhqk,bkhd->bqhd', p, vb))
    return jnp.concatenate(outs, axis=1)


def gla_chunk_causal(q, k, v, log_a):
    B, S, H, Dk = q.shape
    Dv = v.shape[-1]
    NC = S // CHUNK
    qc = q.reshape(B, NC, CHUNK, H, Dk).astype(jnp.float32) * (Dk ** -0.5)
    kc = k.reshape(B, NC, CHUNK, H, Dk).astype(jnp.float32)
    vc = v.reshape(B, NC, CHUNK, H, Dv).astype(jnp.float32)
    G = jnp.cumsum(log_a.reshape(B, NC, CHUNK, H, Dk), axis=2)
    G_tot = G[:, :, -1]
    k_dec = kc * jnp.exp(G_tot[:, :, None] - G)
    delta = jnp.einsum('bnchk,bnchv->bnhkv', k_dec, vc)
    chunk_decay = jnp.exp(G_tot)

    def step(state, inp):
        dec, d = inp
        state = dec[..., None] * state + d
        return state, state

    s0 = jnp.zeros((B, H, Dk, Dv), jnp.float32)
    _, states = lax.scan(step, s0, (jnp.moveaxis(chunk_decay, 1, 0), jnp.moveaxis(delta, 1, 0)))
    states = jnp.moveaxis(states, 0, 1)
    o = jnp.einsum('bnchk,bnhkv->bnchv', qc, states)
    return o.reshape(B, S, H, Dv).astype(v.dtype)


def setup_inputs(seed: int = 0) -> dict:
    key = jax.random.key(seed)
    ks = jax.random.split(key, 18)

    def w(k, shape, fan_in):
        return jax.random.normal(k, shape, jnp.float32) * fan_in ** -0.5

    def gain(k, shape):
        return 1.0 + 0.05 * jax.random.normal(k, shape, jnp.float32)

    L = DEPTH
    return {
        "x": jax.random.normal(ks[0], (BATCH, SEQ, D_MODEL), jnp.float32),
        "ffn1_norm": gain(ks[1], (L, D_MODEL)),
        "ffn1_w_gate": w(ks[2], (L, D_MODEL, D_FF), D_MODEL),
        "ffn1_w_up": w(ks[3], (L, D_MODEL, D_FF), D_MODEL),
        "ffn1_w_down": w(ks[4], (L, D_FF, D_MODEL), D_FF),
        "mix_norm": gain(ks[5], (L, D_MODEL)),
        "w_in": w(ks[6], (L, D_MODEL, IN_WIDTH), D_MODEL),
        "fox_forget_bias": jax.random.uniform(ks[7], (L, FOX_HEADS), jnp.float32, 1.0, 5.0),
        "gla_w_gate_up": w(ks[8], (L, GLA_GATE_RANK, GLA_KEY_WIDTH), GLA_GATE_RANK),
        "gla_gate_bias": 0.1 * jax.random.normal(ks[9], (L, GLA_KEY_WIDTH), jnp.float32),
        "gla_out_norm": gain(ks[10], (L, GLA_DV)),
        "w_out": w(ks[11], (L, MIX_WIDTH, D_MODEL), MIX_WIDTH),
        "ffn2_norm": gain(ks[12], (L, D_MODEL)),
        "ffn2_w_gate": w(ks[13], (L, D_MODEL, D_FF), D_MODEL),
        "ffn2_w_up": w(ks[14], (L, D_MODEL, D_FF), D_MODEL),
        "ffn2_w_down": w(ks[15], (L, D_FF, D_MODEL), D_FF),
        "final_norm": gain(ks[16], (D_MODEL,)),
    }


def reference(x, ffn1_norm, ffn1_w_gate, ffn1_w_up, ffn1_w_down, mix_norm, w_in,
              fox_forget_bias, gla_w_gate_up, gla_gate_bias, gla_out_norm, w_out,
              ffn2_norm, ffn2_w_gate, ffn2_w_up, ffn2_w_down, final_norm):
    B, S, _ = x.shape
    split_idx = [int(i) for i in np.cumsum(IN_SPLITS)[:-1]]
    for l in range(DEPTH):
        h = rms_norm(x, ffn1_norm[l])
        x = x + 0.5 * swiglu(h, ffn1_w_gate[l], ffn1_w_up[l], ffn1_w_down[l])

        h = rms_norm(x, mix_norm[l])
        proj = h @ w_in[l]
        fq, fk, fv, ff, gq, gk, gv, g_low, g_out = jnp.split(proj, split_idx, axis=-1)

        log_f = jax.nn.log_sigmoid((ff + fox_forget_bias[l]).astype(jnp.float32))
        fox = fox_attention(fq.reshape(B, S, FOX_HEADS, FOX_HEAD_DIM),
                            fk.reshape(B, S, FOX_HEADS, FOX_HEAD_DIM),
                            fv.reshape(B, S, FOX_HEADS, FOX_HEAD_DIM), log_f)
        fox = fox.reshape(B, S, FOX_WIDTH)

        log_a = jax.nn.log_sigmoid((g_low @ gla_w_gate_up[l] + gla_gate_bias[l]).astype(jnp.float32)) / GLA_GATE_TEMP
        gla = gla_chunk_causal(gq.reshape(B, S, GLA_HEADS, GLA_DK),
                               gk.reshape(B, S, GLA_HEADS, GLA_DK),
                               gv.reshape(B, S, GLA_HEADS, GLA_DV),
                               log_a.reshape(B, S, GLA_HEADS, GLA_DK))
        gla = rms_norm(gla, gla_out_norm[l]) * jax.nn.silu(g_out.reshape(B, S, GLA_HEADS, GLA_DV))
        gla = gla.reshape(B, S, GLA_VAL_WIDTH)

        x = x + jnp.concatenate([fox, gla], axis=-1) @ w_out[l]

        h = rms_norm(x, ffn2_norm[l])
        x = x + 0.5 * swiglu(h, ffn2_w_gate[l], ffn2_w_up[l], ffn2_w_down[l])
    return rms_norm(x, final_norm)
```

```python
import contextlib
import numpy as np
import concourse.bass as bass
import concourse.mybir as mybir
from concourse.bass_utils import run_bass_kernel_spmd

F32 = mybir.dt.float32
BF16 = mybir.dt.bfloat16
AF = mybir.ActivationFunctionType
ALU = mybir.AluOpType

N_CORES = 8
D = 1024
S = 2048
NB = 2
DFF = 2816
NFF = 22
TT = 16
EPS = 1e-6
GROUPS = [list(range(0, 6)), list(range(6, 12)), list(range(12, 17)), list(range(17, 22))]

DEBUG = False
STOP_AFTER = None
MIX_STOP = None
GLA_STAGES = None
B2_VAR = 0

ENGINES = ("pe", "act", "dve", "pool", "sp")


class Op:
    __slots__ = ("eng", "fn", "reads", "writes", "dma_key", "stream", "idx",
                 "deps", "waits", "signal", "count", "clock", "gidx")

    def __init__(self, eng, fn, reads, writes, dma_key):
        self.eng = eng
        self.fn = fn
        self.reads = reads
        self.writes = writes
        self.dma_key = dma_key
        self.stream = ("dma:" + str(dma_key)) if dma_key is not None else eng
        self.idx = -1
        self.deps = []
        self.waits = []
        self.signal = False
        self.count = 0
        self.clock = None


class Prog:
    def __init__(self, nc):
        self.nc = nc
        self.ops = []

    def op(self, eng, fn, reads=(), writes=(), dma_key=None):
        o = Op(eng, fn, tuple(reads), tuple(writes), dma_key)
        o.gidx = len(self.ops)
        self.ops.append(o)
        return o

    def pe(self, fn, reads=(), writes=()):
        return self.op("pe", fn, reads, writes)

    def act(self, fn, reads=(), writes=()):
        return self.op("act", fn, reads, writes)

    def dve(self, fn, reads=(), writes=()):
        return self.op("dve", fn, reads, writes)

    def pool(self, fn, reads=(), writes=()):
        return self.op("pool", fn, reads, writes)

    def dma(self, eng, key, fn, reads=(), writes=()):
        return self.op(eng, fn, reads, writes, dma_key=key)

    def finalize(self, stack):
        nc = self.nc
        ops = self.ops
        stream_ops = {}
        for o in ops:
            lst = stream_ops.setdefault(o.stream, [])
            o.idx = len(lst)
            lst.append(o)
        last_writer = {}
        readers = {}
        for o in ops:
            deps = {}
            for r in o.reads:
                w = last_writer.get(r)
                if w is not None:
                    deps[id(w)] = w
            for r in o.writes:
                w = last_writer.get(r)
                if w is not None:
                    deps[id(w)] = w
                for rd in readers.get(r, ()):
                    deps[id(rd)] = rd
            for r in o.reads:
                readers.setdefault(r, []).append(o)
            for r in o.writes:
                last_writer[r] = o
                readers[r] = []
            deps.pop(id(o), None)
            o.deps = list(deps.values())
        seen = {e: {} for e in ENGINES}
        for o in ops:
            if o.dma_key is not None:
                o.signal = True
            s = seen[o.eng]
            for d in sorted(o.deps, key=lambda d: -d.gidx):
                if d.stream == "pe" and o.eng == "pe" and o.dma_key is None:
                    continue
                if s.get(d.stream, -1) >= d.idx:
                    continue
                o.waits.append(d)
                d.signal = True
                for k, v in d.clock.items():
                    if s.get(k, -1) < v:
                        s[k] = v
            clk = dict(s)
            clk[o.stream] = o.idx
            o.clock = clk
            if o.dma_key is None and o.eng == "pe":
                s[o.stream] = o.idx
        for st, lst in stream_ops.items():
            c = 0
            for o in lst:
                if o.signal:
                    c += 1
                o.count = c
        sems = {}
        n = 0
        for st, lst in stream_ops.items():
            if any(o.signal for o in lst):
                sems[st] = stack.enter_context(nc.semaphore("sem%d" % n))
                n += 1
        self.n_sems = n
        per_eng = {e: [o for o in ops if o.eng == e] for e in ENGINES}
        block = stack.enter_context(nc.Block())

        def emit(engine, lst):
            for o in lst:
                for d in o.waits:
                    mult = 16 if d.dma_key is not None else 1
                    engine.wait_ge(sems[d.stream], d.count * mult)
                ins = o.fn(engine)
                if o.signal:
                    assert ins is not None, "signalling op must return its instruction"
                    ins.then_inc(sems[o.stream], 16 if o.dma_key is not None else 1)

        @block.sync
        def _(e):
            emit(e, per_eng["sp"])

        @block.tensor
        def _(e):
            emit(e, per_eng["pe"])

        @block.scalar
        def _(e):
            emit(e, per_eng["act"])

        @block.vector
        def _(e):
            emit(e, per_eng["dve"])

        @block.gpsimd
        def _(e):
            emit(e, per_eng["pool"])


def MM(out, pairs, start=True, stop=True):
    pairs = list(pairs)

    def fn(e):
        ins = None
        n = len(pairs)
        for i, (l, r) in enumerate(pairs):
            ins = e.matmul(out, lhsT=l, rhs=r, start=(start and i == 0), stop=(stop and i == n - 1))
        return ins
    return fn


def MMS(items):
    items = [(o, list(p)) for o, p in items]

    def fn(e):
        ins = None
        for out, pairs in items:
            n = len(pairs)
            for i, (l, r) in enumerate(pairs):
                ins = e.matmul(out, lhsT=l, rhs=r, start=(i == 0), stop=(i == n - 1))
        return ins
    return fn


def TRS(items, ident):
    items = list(items)

    def fn(e):
        ins = None
        for out, in_ in items:
            ins = e.transpose(out=out, in_=in_, identity=ident)
        return ins
    return fn


def ACTF(out, in_, func, bias=None, scale=None, accum_out=None):
    def fn(e):
        kw = {}
        if bias is not None:
            kw["bias"] = bias
        if scale is not None:
            kw["scale"] = scale
        if accum_out is not None:
            kw["accum_out"] = accum_out
        return e.activation(out=out, in_=in_, func=func, **kw)
    return fn


def COPY(out, in_):
    return lambda e: e.tensor_copy(out=out, in_=in_)


def ACOPY(out, in_):
    return lambda e: e.copy(out=out, in_=in_)


def AMUL(out, in_, m):
    return lambda e: e.mul(out=out, in_=in_, mul=m)


def TT_(out, in0, in1, op):
    return lambda e: e.tensor_tensor(out=out, in0=in0, in1=in1, op=op)


def TS(out, in0, s1, s2, op0, op1=None):
    if op1 is None:
        return lambda e: e.tensor_single_scalar(out=out, in_=in0, scalar=s1, op=op0)
    return lambda e: e.tensor_scalar(out=out, in0=in0, scalar1=s1, scalar2=s2, op0=op0, op1=op1)


def STT(out, in0, scalar, in1, op0, op1):
    return lambda e: e.scalar_tensor_tensor(out=out, in0=in0, scalar=scalar, in1=in1, op0=op0, op1=op1)


def MEMSET(ap, v):
    return lambda e: e.memset(ap, v)


def DMA(out, in_):
    return lambda e: e.dma_start(out=out, in_=in_)


def blk(slot, c0, n):
    return [(slot, b) for b in range(c0 // 512, (c0 + n - 1) // 512 + 1)]


def build_nc():
    nc = bass.Bass("TRN2", target_bir_lowering=False)
    dt = lambda name, shape, dtype=F32, kind="ExternalInput": nc.dram_tensor(name, shape, dtype, kind=kind).ap()
    x_d = dt("x", [NB, S, D])
    gains_d = dt("gains", [4, 128, D])
    wgu_d = [dt("wgu%d" % f, [NFF, 128, 2 * 8 * 128]) for f in range(2)]
    wd_d = [dt("wd%d" % f, [NFF, 128, D]) for f in range(2)]
    wfox_d = dt("wfox", [8, 128, 8 * 256])
    wmisc_d = dt("wmisc", [128, 8 * 40])
    wgla_d = dt("wgla", [6, 128, 8 * 256])
    fb_d = dt("fb", [8, 1])
    wg2a_d = dt("wg2a", [17, 256])
    gon_d = dt("gon", [128, 1])
    wout_d = dt("wout", [128, 8 * D])
    y_d = dt("y", [NB, S, D], kind="ExternalOutput")
    if DEBUG:
        dbg_d = dt("dbg", [8, 128, 16 * D], kind="ExternalOutput")

    with contextlib.ExitStack() as st:
        sb = lambda name, shape, dtype: st.enter_context(nc.sbuf_tensor(name, shape, dtype))
        x_sb = sb("x_sb", [128, TT, D], F32)
        hT = sb("hT", [128, 8, S], BF16)
        gt = sb("gt", [128, D], F32)
        hbf = sb("hbf", [128, 2, D], BF16)
        junk = sb("junk", [128, D], BF16)
        stats = sb("stats", [128, 64], F32)
        ident = sb("ident", [128, 128], BF16)
        ones_bf = sb("ones_bf", [128, 128], BF16)
        mask01 = sb("mask01", [128, 128], BF16)
        tri = sb("tri", [128, 128], BF16)
        ind = sb("ind", [128, 2], BF16)
        ones_f = sb("ones_f", [128, 64], F32)
        cst = sb("cst", [128, 4], F32)
        scan1 = sb("scan1", [8, 512], F32)
        fb = sb("fbt", [8, 1], F32)
        gon = sb("gont", [128, 1], F32)
        wg2a = sb("wg2at", [32, 256], BF16)
        wmisc = sb("wmisct", [128, 8, 40], BF16)
        pieces = sb("pieces", [72, S], BF16)
        fcar = sb("fcar", [8, 2], F32)
        decs = sb("decs", [128, TT * 4], F32)
        A = sb("slotA", [128, 24576], BF16)
        B = sb("slotB", [128, 14336], BF16)
        C = sb("slotC", [128, 8192], BF16)
        banks = [st.enter_context(nc.psum_tensor("bank%d" % i, [128, 512], F32)) for i in range(8)]
        PS = lambda b: ("ps", b)

        eps_ap = cst[:, 0:1]
        one_ap = cst[:, 1:2]

        P = Prog(nc)

        P.pool(MEMSET(ident[:], 0.0), writes=["ident"])
        P.pool(lambda e: e.affine_select(out=ident[:], in_=ident[:], pattern=[[-1, 128]], compare_op=ALU.not_equal,
                                         fill=1.0, base=0, channel_multiplier=1), writes=["ident"])
        P.pool(MEMSET(ones_bf[:], 1.0), writes=["ones_bf"])
        P.pool(MEMSET(ones_f[:], 1.0), writes=["ones_f"])
        P.pool(MEMSET(mask01[:], 1.0), writes=["mask01"])
        P.pool(lambda e: e.affine_select(out=mask01[:], in_=mask01[:], pattern=[[1, 128]], compare_op=ALU.is_ge,
                                         fill=0.0, base=0, channel_multiplier=-1), writes=["mask01"])
        P.pool(MEMSET(tri[:], 1.0 / 16.0), writes=["tri"])
        P.pool(lambda e: e.affine_select(out=tri[:], in_=tri[:], pattern=[[-1, 128]], compare_op=ALU.is_gt,
                                         fill=0.0, base=0, channel_multiplier=1), writes=["tri"])
        P.pool(MEMSET(tri[64:128, 0:64], 0.0), writes=["tri"])
        P.pool(MEMSET(ind[:], 0.0), writes=["ind"])
        P.pool(MEMSET(ind[0:64, 0:1], 1.0 / 16.0), writes=["ind"])
        P.pool(MEMSET(ind[64:128, 1:2], 1.0 / 16.0), writes=["ind"])
        P.pool(MEMSET(cst[:, 0:1], EPS), writes=["cst"])
        P.pool(MEMSET(cst[:, 1:2], 1.0), writes=["cst"])
        P.pool(MEMSET(scan1[:], 1.0), writes=["scan1"])
        P.pool(MEMSET(wg2a[:], 0.0), writes=["wg2a"])
        P.dma("sp", "c_fb", DMA(fb[:], fb_d[:, :]), writes=["fb"])
        P.dma("sp", "c_gon", DMA(gon[:], gon_d[:, :]), writes=["gon"])
        P.dma("pool", "c_wg2a", DMA(wg2a[0:17, :], wg2a_d[:, :]), reads=["wg2a"], writes=["wg2a"])
        P.dma("pool", "c_wmisc", DMA(wmisc[:].rearrange("p k c -> p (k c)"), wmisc_d[:, :]), writes=["wmisc"])

        dbg_n = [0]

        def dbg_dump(ap_src, ncols, reads):
            if not DEBUG:
                return
            i = dbg_n[0]
            dbg_n[0] += 1
            P.dma("sp", "dbg%d" % i, DMA(dbg_d[i, :, 0:ncols], ap_src), reads=reads, writes=[("dbg", i)])

        ssq = stats[:, 0:16]
        lnv = stats[:, 16:32]
        rstd = stats[:, 32:48]

        def norm_begin(gain_idx):
            P.dma("sp", "gain", DMA(gt[:], gains_d[gain_idx]), writes=["gt"])
            P.dve(MEMSET(ssq, 0.0), writes=[("ssq", g) for g in range(4)])

        def norm_stats(tg):
            for i in range(4 * tg, 4 * tg + 4):
                P.act(ACTF(junk[:], x_sb[:, i, :], AF.Square, accum_out=ssq[:, i:i + 1]),
                      reads=[("x", i, 0), ("x", i, 1)], writes=["junk", ("ssq", tg)])
            c = slice(4 * tg, 4 * tg + 4)
            P.act(ACTF(lnv[:, c], ssq[:, c], AF.Ln, bias=eps_ap, scale=1.0 / D), reads=[("ssq", tg), "cst"],
                  writes=[("lnv", tg)])
            P.act(ACTF(rstd[:, c], lnv[:, c], AF.Exp, scale=-0.5), reads=[("lnv", tg)], writes=[("rstd", tg)])

        def norm_apply(s, tg, final):
            for i in range(4 * tg, 4 * tg + 4):
                if final:
                    ob = C[:, 4096 + (i % 2) * 2048: 4096 + (i % 2 + 1) * 2048].bitcast(F32)
                    okeys = blk("C", 4096 + (i % 2) * 2048, 2048)
                    P.dve(STT(ob, x_sb[:, i, :], rstd[:, i:i + 1], gt[:], ALU.mult, ALU.mult),
                          reads=[("x", i, 0), ("x", i, 1), ("rstd", tg), "gt"], writes=okeys)
                    P.dma("sp", ("y", i % 2), DMA(y_d[s, i * 128:(i + 1) * 128, :], ob), reads=okeys,
                          writes=[("y", s, i)])
                    continue
                hb = hbf[:, i % 2, :]
                P.dve(STT(hb, x_sb[:, i, :], rstd[:, i:i + 1], gt[:], ALU.mult, ALU.mult),
                      reads=[("x", i, 0), ("x", i, 1), ("rstd", tg), "gt"], writes=[("hbf", i % 2)])
                bk = 6 + (i % 2)
                pT = banks[bk][:].bitcast(BF16).rearrange("p (k t) -> p k t", k=8)
                P.pe(TRS([(pT[:, k, :], hb[:, k * 128:(k + 1) * 128]) for k in range(8)], ident[:]),
                     reads=[("hbf", i % 2), "ident"], writes=[PS(bk)])
                P.act(ACOPY(hT[:, :, i * 128:(i + 1) * 128], pT), writes=[PS(bk), ("hT", i)])

        def make_norm_hook(s, final):
            def hook(i):
                if i % 4 != 3:
                    return
                tg = i // 4
                norm_stats(tg)
                if final:
                    norm_apply(s, tg, True)
                    return
                if tg >= 1:
                    norm_apply(s, tg - 1, final)
                if tg == 3:
                    norm_apply(s, 3, final)
            return hook

        def norm_phase_load(s, gain_idx):
            norm_begin(gain_idx)
            hook = make_norm_hook(s, False)
            for i in range(TT):
                P.dma("pool" if s > 0 else "sp", ("x", i), DMA(x_sb[:, i, :], x_d[s, i * 128:(i + 1) * 128, :]),
                      writes=[("x", i, 0), ("x", i, 1)])
                hook(i)

        def ffn_phase(f, hook=None, split_first=False, split_last=False):
            wcnt = [0]
            for gi, chunks in enumerate(GROUPS):
                abuf = gi % 2
                a0 = abuf * 12288
                n = len(chunks)
                split = (split_first and gi == 0) or (split_last and gi == len(GROUPS) - 1)
                halves = [(0, 1), (2, 3)] if split else [(0, 1, 2, 3)]
                for hi, tgs in enumerate(halves):
                    for cl, c in enumerate(chunks):
                        wb = wcnt[0] % 3
                        wcnt[0] += 1
                        wv = C[:, wb * 2048:(wb + 1) * 2048]
                        wkeys = blk("C", wb * 2048, 2048)
                        P.dma("pool", ("wgu", wb), DMA(wv, wgu_d[f][c]), writes=wkeys)
                        w4 = wv.rearrange("p (g k f) -> p g k f", g=2, k=8)
                        if hi == 0:
                            dv = B[:, abuf * 6144 + cl * 1024: abuf * 6144 + (cl + 1) * 1024]
                            dkeys = blk("B", abuf * 6144 + cl * 1024, 1024)
                            P.dma("pool", ("wd", abuf, cl), DMA(dv, wd_d[f][c]), writes=dkeys)
                        for tg in tgs:
                            par = tg % 2
                            hk = [("hT", 4 * tg + j) for j in range(4)]
                            P.pe(MM(banks[par][:], [(w4[:, 0, k, :], hT[:, k, tg * 512:(tg + 1) * 512]) for k in range(8)]),
                                 reads=wkeys + hk, writes=[PS(par)])
                            P.pe(MM(banks[2 + par][:], [(w4[:, 1, k, :], hT[:, k, tg * 512:(tg + 1) * 512]) for k in range(8)]),
                                 reads=wkeys + hk, writes=[PS(2 + par)])
                            sgc = 6144 + par * 512
                            sg = C[:, sgc:sgc + 512]
                            P.act(ACTF(sg, banks[par][:], AF.Silu), writes=[PS(par)] + blk("C", sgc, 512))
                            ac = a0 + cl * 2048 + tg * 512
                            P.dve(TT_(A[:, ac:ac + 512], banks[2 + par][:], sg, ALU.mult),
                                  reads=blk("C", sgc, 512), writes=[PS(2 + par)] + blk("A", ac, 512))
                    for i in range(4 * tgs[0], 4 * tgs[-1] + 4):
                        for dh in range(2):
                            bk = 4 + ((2 * i + dh) % 2)
                            pairs = []
                            rk = []
                            for cl in range(n):
                                ac = a0 + cl * 2048 + i * 128
                                dc = abuf * 6144 + cl * 1024 + dh * 512
                                pairs.append((A[:, ac:ac + 128], B[:, dc:dc + 512]))
                                rk += blk("A", ac, 128) + blk("B", dc, 512)
                            P.pe(MM(banks[bk][:], pairs), reads=rk, writes=[PS(bk)])
                            xv = x_sb[:, i, dh * 512:(dh + 1) * 512]
                            P.dve(STT(xv, banks[bk][:], 0.5, xv, ALU.mult, ALU.add), writes=[PS(bk), ("x", i, dh)])
                        if hook is not None and gi == len(GROUPS) - 1:
                            hook(i)

        def mix_phase(s, hook=None):
            concat0 = 0
            GK0 = 0
            GQ0 = 4096
            GV0 = 16384
            WO0 = 16384

            def load_wchunk(b, src, ncols):
                wv = C[:, b * 2048:b * 2048 + ncols]
                keys = blk("C", b * 2048, 2048)
                P.dma("pool", ("wchunk", b), DMA(wv, src), writes=keys)
                return wv.rearrange("p (k c) -> p k c", k=8), keys

            wc = [0]

            def next_wchunk(src, ncols=2048):
                b = wc[0] % 2
                wc[0] += 1
                return load_wchunk(b, src, ncols)

            for half in range(2):
                w3, wk = next_wchunk(wgla_d[4 + half])
                for hh in range(2):
                    h = 2 * half + hh
                    for tg in range(4):
                        bk = (hh * 4 + tg) % 2
                        hk = [("hT", 4 * tg + j) for j in range(4)]
                        P.pe(MM(banks[bk][:], [(w3[:, k, hh * 128:(hh + 1) * 128], hT[:, k, tg * 512:(tg + 1) * 512])
                                               for k in range(8)]), reads=wk + hk, writes=[PS(bk)])
                        oc = concat0 + (4 + h) * 2048 + tg * 512
                        P.act(ACTF(A[:, oc:oc + 512], banks[bk][:], AF.Silu), writes=[PS(bk)] + blk("A", oc, 512))
                        P.dve(TS(A[:, oc:oc + 512], A[:, oc:oc + 512], gon[:, 0:1], None, ALU.mult), reads=["gon"],
                              writes=blk("A", oc, 512))
            w3, wk = next_wchunk(wgla_d[1])
            for i in range(TT):
                bk = i % 2
                P.pe(MM(banks[bk][:, 0:256], [(hT[:, k, i * 128:(i + 1) * 128], w3[:, k, :]) for k in range(8)]),
                     reads=wk + [("hT", i)], writes=[PS(bk)])
                oc = GK0 + i * 256
                P.dve(COPY(A[:, oc:oc + 256], banks[bk][:, 0:256]), writes=[PS(bk)] + blk("A", oc, 256))
            def proj_units():
                w3, wk = next_wchunk(wgla_d[0])
                for j in range(2):
                    for tg in range(4):
                        def u(j=j, tg=tg, w3=w3, wk=wk):
                            bk = 3 + ((j * 4 + tg) % 2)
                            hk = [("hT", 4 * tg + jj) for jj in range(4)]
                            P.pe(MM(banks[bk][:], [(w3[:, k, j * 128:(j + 1) * 128], hT[:, k, tg * 512:(tg + 1) * 512])
                                                   for k in range(8)]), reads=wk + hk, writes=[PS(bk)])
                            oc = GQ0 + j * 2048 + tg * 512
                            P.act(AMUL(A[:, oc:oc + 512], banks[bk][:], 0.125), writes=[PS(bk)] + blk("A", oc, 512))
                        yield u
                for half in range(2):
                    w3, wk = next_wchunk(wgla_d[2 + half])
                    for i in range(TT):
                        def u(half=half, i=i, w3=w3, wk=wk):
                            bk = 3 + (i % 2)
                            P.pe(MM(banks[bk][:, 0:256], [(hT[:, k, i * 128:(i + 1) * 128], w3[:, k, :]) for k in range(8)]),
                                 reads=wk + [("hT", i)], writes=[PS(bk)])
                            oc = GV0 + i * 512 + half * 256
                            P.act(ACOPY(A[:, oc:oc + 256], banks[bk][:, 0:256]), writes=[PS(bk)] + blk("A", oc, 256))
                        yield u

            if MIX_STOP == 1:
                return
            def Bf(c0, n):
                return B[:, 2 * c0:2 * (c0 + n)].bitcast(F32), blk("B", 2 * c0, 2 * n)

            def Bv(c0, n, dtype=BF16):
                v = B[:, c0:c0 + n]
                return (v.bitcast(F32) if dtype == F32 else v), blk("B", c0, n)

            def a_v(par): return Bv(par * 512, 512, F32)
            def lsg_v(par): return Bv(1024 + par * 256, 256)
            def eR_v(par): return Bv(1536 + par * 544, 544, F32)
            def S_v(par): return Bv(3072 + par * 1024, 1024, F32)
            def Sbf_v(par): return Bv(5120 + par * 512, 512)
            def osq_v(par): return Bv(6144 + par * 512, 512)
            rs_v = Bv(7168, 1024, F32)
            t1_v = Bv(8192, 1024, F32)
            def glT_v(par): return Bv(9216 + par * 512, 512)
            def Cf(c0, n):
                return C[:, 4096 + 2 * c0: 4096 + 2 * (c0 + n)].bitcast(F32), blk("C", 4096 + 2 * c0, 2 * n)
            fz_v = Cf(0, 512)
            fa_v = Cf(512, 512)
            fc_v = Cf(1024, 512)
            fr_v = Cf(1536, 512)

            bmask = junk[:, 0:512]
            P.pool(MEMSET(bmask, 0.0), writes=["junk"])
            for j in range(2):
                P.pool(MEMSET(junk[0:64, j * 256:j * 256 + 128], 1.0), writes=["junk"])
                P.pool(MEMSET(junk[64:128, j * 256 + 128:j * 256 + 256], 1.0), writes=["junk"])
            Sinit, Sinit_k = S_v(1)
            P.dve(MEMSET(Sinit, 0.0), writes=Sinit_k)

            def tgroup_prologue(tg):
                par = tg % 2
                hk = [("hT", 4 * tg + j) for j in range(4)]
                P.pe(MM(banks[1][0:40, :], [(wmisc[:, k, :], hT[:, k, tg * 512:(tg + 1) * 512]) for k in range(8)]),
                     reads=["wmisc"] + hk, writes=[PS(1)])
                g, gk_ = glT_v(par)
                P.dve(MEMSET(g[0:32, :], 1.0), writes=gk_)
                P.dve(COPY(g[0:16, :], banks[1][0:16, :]), writes=[PS(1)] + gk_)
                fz, fzk = fz_v
                fa, fak = fa_v
                fc, fck = fc_v
                fr, frk = fr_v
                P.dve(TS(fz[0:8, :], banks[1][32:40, :], fb[0:8, 0:1], None, ALU.add), reads=["fb"],
                      writes=[PS(1)] + fzk)
                P.dve(STT(fa[0:8, :], fz[0:8, :], -1.0, fz[0:8, :], ALU.mult, ALU.max), reads=fzk, writes=fak)
                P.act(ACTF(fa[0:8, :], fa[0:8, :], AF.Exp, scale=-1.0), writes=fak)
                P.act(ACTF(fa[0:8, :], fa[0:8, :], AF.Ln, bias=one_ap[0:8, :]), reads=["cst"], writes=fak)
                P.dve(STT(fz[0:8, :], fz[0:8, :], 0.0, fa[0:8, :], ALU.min, ALU.subtract), reads=fak, writes=fzk)
                if tg == 0:
                    P.dve(lambda e: e.tensor_tensor_scan(out=fc[0:8, :], data0=scan1[:], data1=fz[0:8, :], initial=0.0,
                                                         op0=ALU.mult, op1=ALU.add),
                          reads=fzk + ["scan1"], writes=fck)
                else:
                    P.dve(lambda e: e.tensor_tensor_scan(out=fc[0:8, :], data0=scan1[:], data1=fz[0:8, :],
                                                         initial=fcar[0:8, 0:1], op0=ALU.mult, op1=ALU.add),
                          reads=fzk + ["scan1", "fcar"], writes=fck)
                P.dve(COPY(fcar[0:8, 0:1], fc[0:8, 511:512]), reads=fck, writes=["fcar"])
                tr = slice(tg * 512, (tg + 1) * 512)
                pk = [("pieces", tg)]
                P.dve(COPY(pieces[0:8, tr], fc[0:8, :]), reads=fck, writes=pk)
                P.dve(TT_(fr[0:8, :], fc[0:8, :], pieces[0:8, tr], ALU.subtract), reads=fck + pk, writes=frk)
                midt = C[:, 5120:5632]
                P.dve(COPY(midt[0:8, :], fr[0:8, :]), reads=frk, writes=fak)
                P.dve(COPY(pieces[32:40, tr], midt[0:8, :]), reads=fak, writes=pk)
                P.dve(TT_(fc[0:8, :], fr[0:8, :], midt[0:8, :], ALU.subtract), reads=frk + fak, writes=fck)
                P.dve(COPY(pieces[64:72, tr], fc[0:8, :]), reads=fck, writes=pk)

            def stage_A1(t):
                par = t % 2
                tg = t // 4
                g, gk_ = glT_v(tg % 2)
                c0 = (t % 4) * 128
                P.pe(MM(banks[0][:, 0:256], [(g[0:17, c0:c0 + 128], wg2a[0:17, :])]),
                     reads=gk_ + ["wg2a"], writes=[PS(0)])
                a, ak = a_v(par)
                l, lk = lsg_v(par)
                P.act(ACTF(a, banks[0][:, 0:256], AF.Abs), writes=[PS(0)] + ak)
                P.act(ACTF(a, a, AF.Exp, scale=-1.0), writes=ak)
                P.act(ACTF(a, a, AF.Ln, bias=one_ap), reads=["cst"], writes=ak)
                P.dve(STT(l, banks[0][:, 0:256], 0.0, a, ALU.min, ALU.subtract), reads=ak, writes=[PS(0)] + lk)

            def stage_A2(t):
                par = t % 2
                l, lk = lsg_v(par)
                e_, ek = eR_v(par)
                P.pe(MMS([(banks[2][:, 0:256], [(tri[:], l)]),
                          (banks[2][:, 256:258], [(l[:, 0:128], ind[:])]),
                          (banks[2][:, 258:260], [(l[:, 128:256], ind[:])])]),
                     reads=lk + ["tri", "ind"], writes=[PS(2)])
                P.act(ACTF(e_[:, 0:256], banks[2][:, 0:256], AF.Exp), writes=[PS(2)] + ek)
                P.act(ACTF(decs[:, 4 * t:4 * t + 4], banks[2][:, 256:260], AF.Exp), writes=[PS(2), ("decs", t)])
                kc = GK0 + t * 256
                P.pool(TT_(A[:, kc:kc + 256], A[:, kc:kc + 256], e_[:, 0:256], ALU.mult), reads=ek, writes=blk("A", kc, 256))

            def stage_B1(t):
                kc = GK0 + t * 256
                vc = GV0 + t * 512
                for c in range(2):
                    par = c
                    pb_ = 3 + par
                    rows = slice(c * 64, (c + 1) * 64)
                    P.pe(MMS([(banks[pb_][:, j * 256:(j + 1) * 256],
                               [(A[rows, kc + j * 128: kc + (j + 1) * 128], A[rows, vc + j * 256: vc + (j + 1) * 256])])
                              for j in range(2)]),
                         reads=blk("A", kc, 256) + blk("A", vc, 512), writes=[PS(pb_)])
                for c in range(2):
                    par = c
                    pb_ = 3 + par
                    Sn, Snk = S_v(par)
                    Sp, Spk = S_v(1 - par)
                    for j in range(2):
                        P.dve(STT(Sn[:, j * 256:(j + 1) * 256], Sp[:, j * 256:(j + 1) * 256],
                                  decs[:, 4 * t + 2 * j + c: 4 * t + 2 * j + c + 1],
                                  banks[pb_][:, j * 256:(j + 1) * 256], ALU.mult, ALU.add),
                              reads=Spk + [("decs", t)], writes=[PS(pb_)] + Snk)
                    sb_, sbk = Sbf_v(par)
                    P.dve(TT_(sb_, Sn, bmask, ALU.mult), reads=Snk + ["junk"], writes=sbk)

            def po_bank(t):
                return 5 if t % 2 == 0 else 7

            def stage_B2(t):
                items = []
                rk = []
                bk = po_bank(t)
                for c in range(2):
                    sb_, sbk = Sbf_v(c)
                    rk += sbk
                    for h in range(4):
                        j, hh = h // 2, h % 2
                        lhsT = sb_[:, j * 256 + hh * 128: j * 256 + (hh + 1) * 128]
                        qc = GQ0 + j * 2048 + t * 128 + c * 64
                        rk += blk("A", qc, 64)
                        items.append((banks[bk][:, h * 128 + c * 64: h * 128 + (c + 1) * 64], [(lhsT, A[:, qc:qc + 64])]))
                P.pe(MMS(items), reads=rk, writes=[PS(bk)])

            def stage_B2b(t):
                bk = po_bank(t)
                osq, ok_ = osq_v(t % 2)
                P.act(ACTF(osq, banks[bk][:], AF.Square), writes=[PS(bk)] + ok_)

            def stage_C(t):
                bk = po_bank(t)
                osq, ok_ = osq_v(t % 2)
                rs, rsk = rs_v
                t1, t1k = t1_v
                P.pe(MM(banks[6][:], [(ones_bf[:], osq)]), reads=ok_ + ["ones_bf"], writes=[PS(6)])
                P.act(ACTF(rs, banks[6][:], AF.Ln, bias=eps_ap, scale=1.0 / 128.0), reads=["cst"], writes=[PS(6)] + rsk)
                P.act(ACTF(rs, rs, AF.Exp, scale=-0.5), writes=rsk)
                P.dve(TT_(t1, banks[bk][:], rs, ALU.mult), reads=rsk, writes=[PS(bk)] + t1k)
                gv3 = A[:, concat0 + 4 * 2048: concat0 + 8 * 2048].rearrange("p (h t) -> p h t", h=4)[:, :, t * 128:(t + 1) * 128]
                gkeys = [("A", (concat0 + (4 + h) * 2048 + t * 128) // 512) for h in range(4)]
                P.pool(TT_(gv3, t1.rearrange("p (h t) -> p h t", h=4), gv3, ALU.mult), reads=t1k, writes=gkeys)

            fox_w = {}

            def fox_load(h):
                fox_w[h] = next_wchunk(wfox_d[h], 2048)

            units_it = proj_units()
            n_units = 8 + 2 * TT
            emitted = 0
            for step in range(TT + 1):
                t = step
                if t < TT and t % 4 == 0:
                    tgroup_prologue(t // 4)
                if t < TT:
                    stage_A1(t)
                if 0 <= step - 1 < TT:
                    stage_A2(step - 1)
                want = min(n_units, ((step + 1) * n_units + TT) // (TT + 1))
                while emitted < want:
                    next(units_it)()
                    emitted += 1
            while emitted < n_units:
                next(units_it)()
                emitted += 1
            fox_load(0)
            fox_load(1)
            for step in range(TT + 2):
                if 0 <= step - 1 < TT:
                    stage_B2(step - 1)
                if step < TT:
                    stage_B1(step)
                if 0 <= step - 1 < TT:
                    stage_B2b(step - 1)
                if 0 <= step - 2 < TT:
                    stage_C(step - 2)
            if MIX_STOP == 2:
                return
            for b in range(2):
                q0, k0 = b * 2048, 4096 + b * 2048
                P.pool(MEMSET(B[64:70, q0:q0 + 2048], -1.0),
                       writes=blk("B", q0, 2048) + [("auginit", b)] + [("augq", b, r) for r in range(3)])
                P.pool(MEMSET(B[64:70, k0:k0 + 2048], 1.0),
                       writes=blk("B", k0, 2048) + [("auginit", b)] + [("augk", b, r) for r in range(3)])
                v0 = 8192 + b * 3072
                P.dve(MEMSET(B[:, v0:v0 + 3072].rearrange("p (i c) -> p i c", c=192)[:, :, 64:128], 1.0),
                      writes=blk("B", v0, 3072))

            def proj_qk(h, tg):
                b = h % 2
                q0, k0 = b * 2048, 4096 + b * 2048
                w3, wk = fox_w[h]
                hk = [("hT", 4 * tg + j) for j in range(4)]
                P.pe(MM(banks[6][:], [(w3[:, k, 0:128], hT[:, k, tg * 512:(tg + 1) * 512]) for k in range(8)]),
                     reads=wk + hk, writes=[PS(6)])
                qc = q0 + tg * 512
                kc = k0 + tg * 512
                P.dve(TS(B[0:64, qc:qc + 512], banks[6][0:64, :], 0.125, None, ALU.mult), writes=[PS(6)] + blk("B", qc, 512))
                P.dve(COPY(B[0:64, kc:kc + 512], banks[6][64:128, :]), writes=[PS(6)] + blk("B", kc, 512))

            def proj_v(pair, tg):
                h = 2 * pair
                w3, wk = fox_w[h]
                v0 = 8192 + (pair % 2) * 3072
                v3 = B[:, v0:v0 + 3072].rearrange("p (i c) -> p i c", c=192)
                hk = [("hT", 4 * tg + j) for j in range(4)]
                P.pe(MMS([(banks[7][:, j * 128:(j + 1) * 128],
                           [(hT[:, k, (4 * tg + j) * 128:(4 * tg + j + 1) * 128], w3[:, k, 128:256]) for k in range(8)])
                          for j in range(4)]), reads=wk + hk, writes=[PS(7)])
                p4 = banks[7][:].rearrange("p (j e c) -> p j e c", j=4, e=2)
                vk = blk("B", v0 + 4 * tg * 192, 4 * 192)
                P.dve(COPY(v3[:, 4 * tg:4 * tg + 4, 0:64], p4[:, :, 0, :]), writes=[PS(7)] + vk)
                P.dve(COPY(v3[:, 4 * tg:4 * tg + 4, 128:192], p4[:, :, 1, :]), writes=[PS(7)] + vk)

            def proj_aug(h):
                b = h % 2
                q0, k0 = b * 2048, 4096 + b * 2048
                for r in range(3):
                    P.dma("sp", ("augq", b, r), DMA(B[67 + r:68 + r, q0:q0 + 2048], pieces[32 * r + h:32 * r + h + 1, :]),
                          reads=[("pieces", g_) for g_ in range(4)], writes=[("augq", b, r)])
                    P.dma("sp", ("augk", b, r), DMA(B[64 + r:65 + r, k0:k0 + 2048], pieces[32 * r + h:32 * r + h + 1, :]),
                          reads=[("pieces", g_) for g_ in range(4)], writes=[("augk", b, r)])

            SKEW = 3
            units = []
            for h in range(8):
                for tg in range(4):
                    nj = 4 * tg + 4
                    for j in range(nj):
                        units.append((h, tg, j, nj))
            grp_cnt = {}

            def unit_bufs(idx):
                r = idx % 4
                pr = (0, 1, 2, 5)[r]
                pT0 = 6144 + r * 512
                return pr, pT0

            def emit_qk(idx):
                h, tg, j, nj = units[idx]
                b = h % 2
                q0, k0 = b * 2048, 4096 + b * 2048
                augk = [("augq", b, r) for r in range(3)] + [("augk", b, r) for r in range(3)] + [("auginit", b)]
                t0 = tg * 512
                jj = j - 4 * tg
                off = jj * 128 if jj > 0 else 0
                W = 512 - off
                pr, pT0 = unit_bufs(idx)
                pTk = blk("C", pT0, 512)
                P.pe(MM(banks[pr][:, 0:W], [(B[0:70, k0 + j * 128: k0 + (j + 1) * 128],
                                             B[0:70, q0 + t0 + off: q0 + t0 + 512])]),
                     reads=blk("B", k0 + j * 128, 128) + blk("B", q0 + t0, 512) + augk, writes=[PS(pr)])
                P.act(ACTF(C[:, pT0:pT0 + W], banks[pr][:, 0:W], AF.Exp), writes=[PS(pr)] + pTk)
                if jj >= 0:
                    P.pool(TT_(C[:, pT0:pT0 + 128], C[:, pT0:pT0 + 128], mask01[:], ALU.mult),
                           reads=["mask01"], writes=pTk)

            def emit_pv(idx):
                h, tg, j, nj = units[idx]
                pair = h // 2
                odd = h % 2
                v0 = 8192 + (pair % 2) * 3072
                v3 = B[:, v0:v0 + 3072].rearrange("p (i c) -> p i c", c=192)
                g = (h, tg)
                if g not in grp_cnt:
                    grp_cnt[g] = len(grp_cnt)
                ob = 3 + (grp_cnt[g] % 2)
                jj = j - 4 * tg
                off = jj * 128 if jj > 0 else 0
                W = 512 - off
                pr, pT0 = unit_bufs(idx)
                pTk = blk("C", pT0, 512)
                lhsT = v3[:, j, 64:192] if odd else v3[:, j, 0:128]
                P.pe(MM(banks[ob][:, off:512], [(lhsT, C[:, pT0:pT0 + W])], start=(j == 0), stop=(j == nj - 1)),
                     reads=pTk + blk("B", v0 + j * 192, 192), writes=[PS(ob)])
                if j == nj - 1:
                    rc0 = 4096 + (grp_cnt[g] % 2) * 1024
                    rcp = C[:, rc0:rc0 + 1024].bitcast(F32)
                    rck = blk("C", rc0, 1024)
                    drow = slice(64, 128) if odd else slice(0, 64)
                    srow = slice(0, 64) if odd else slice(64, 128)
                    P.dve(lambda e, o_=rcp[drow, :], i_=banks[ob][srow, :]: e.reciprocal(out=o_, in_=i_),
                          writes=[PS(ob)] + rck)
                    oc = concat0 + (h // 2) * 2048 + tg * 512
                    P.dve(TT_(A[drow, oc:oc + 512], banks[ob][drow, :], rcp[drow, :], ALU.mult), reads=rck,
                          writes=[PS(ob)] + blk("A", oc, 512))

            if MIX_STOP == 3:
                return
            for tg in range(4):
                proj_qk(0, tg)
                proj_v(0, tg)
            proj_aug(0)
            fox_load(2)
            wo3 = A[:, WO0:WO0 + 8192].rearrange("p (k d) -> p k d", k=8)
            wok = blk("A", WO0, 8192)
            P.dma("pool", "wout", DMA(A[:, WO0:WO0 + 8192], wout_d[:, :]), writes=wok)
            if MIX_STOP == 4:
                return
            side = {}
            base = 0
            for h in range(8):
                if h + 1 < 8:
                    for tg in range(4):
                        side.setdefault(base + 4 + 8 * tg, []).append(("qk", h + 1, tg))
                    side.setdefault(base + 5, []).append(("aug", h + 1))
                    if h % 2 == 1:
                        for tg in range(4):
                            side.setdefault(base + 8 + 8 * tg, []).append(("v", (h + 1) // 2, tg))
                    if h + 3 < 8:
                        side.setdefault(base + 34, []).append(("load", h + 3))
                base += 40
            for idx in range(len(units) + SKEW):
                if idx < len(units):
                    emit_qk(idx)
                if idx - SKEW >= 0:
                    emit_pv(idx - SKEW)
                for item in side.get(idx, ()):
                    if item[0] == "qk":
                        proj_qk(item[1], item[2])
                    elif item[0] == "v":
                        proj_v(item[1], item[2])
                    elif item[0] == "aug":
                        proj_aug(item[1])
                    else:
                        fox_load(item[1])

            if MIX_STOP == 5:
                return
            cT = A[:, concat0:concat0 + 16384].rearrange("p (k t) -> p k t", k=8)
            for i in range(TT):
                for dh in range(2):
                    bk = (2 * i + dh) % 2
                    rk = list(wok)
                    for k in range(8):
                        rk += blk("A", concat0 + k * 2048 + i * 128, 128)
                    P.pe(MM(banks[bk][:], [(cT[:, k, i * 128:(i + 1) * 128], wo3[:, k, dh * 512:(dh + 1) * 512])
                                           for k in range(8)]), reads=rk, writes=[PS(bk)])
                    xv = x_sb[:, i, dh * 512:(dh + 1) * 512]
                    P.dve(TT_(xv, banks[bk][:], xv, ALU.add), writes=[PS(bk), ("x", i, dh)])
                if hook is not None:
                    hook(i)

        def dump_x(s):
            for i in range(TT):
                P.dma("sp", ("yd", i % 2), DMA(y_d[s, i * 128:(i + 1) * 128, :], x_sb[:, i, :]),
                      reads=[("x", i, 0), ("x", i, 1)], writes=[("y", s, i)])

        for s in range(NB if STOP_AFTER is None else 1):
            norm_phase_load(s, 0)
            if STOP_AFTER == 0:
                dump_x(s); continue
            if STOP_AFTER == 1:
                ffn_phase(0)
                dump_x(s); continue
            norm_begin(1)
            ffn_phase(0, make_norm_hook(s, False), split_first=True)
            if STOP_AFTER == 2:
                mix_phase(s)
                dump_x(s); continue
            norm_begin(2)
            mix_phase(s, make_norm_hook(s, False))
            if STOP_AFTER == 3:
                ffn_phase(1)
                dump_x(s); continue
            norm_begin(3)
            ffn_phase(1, make_norm_hook(s, True), split_last=True)
        P.op("sp", lambda e: None, reads=[("y", s, i) for s in range(NB if STOP_AFTER is None else 1) for i in range(TT)]
             + ([("dbg", i) for i in range(dbg_n[0])] if DEBUG else []))
        P.finalize(st)
    return nc


def _prep_weights(inp):
    f32 = lambda a: np.ascontiguousarray(np.asarray(a, dtype=np.float32))
    out = {}
    gains = np.stack([np.asarray(inp["ffn1_norm"])[0], np.asarray(inp["mix_norm"])[0],
                      np.asarray(inp["ffn2_norm"])[0], np.asarray(inp["final_norm"])], axis=0)
    out["gains"] = f32(np.broadcast_to(gains[:, None, :], (4, 128, D)))
    for f, pre in enumerate(("ffn1", "ffn2")):
        wg = np.asarray(inp[pre + "_w_gate"])[0]
        wu = np.asarray(inp[pre + "_w_up"])[0]
        wd = np.asarray(inp[pre + "_w_down"])[0]
        g = wg.reshape(8, 128, NFF, 128).transpose(2, 1, 0, 3)
        u = wu.reshape(8, 128, NFF, 128).transpose(2, 1, 0, 3)
        out["wgu%d" % f] = f32(np.stack([g, u], axis=2).reshape(NFF, 128, 2 * 8 * 128))
        out["wd%d" % f] = f32(wd.reshape(NFF, 128, D))
    win = np.asarray(inp["w_in"])[0]
    w3 = win.reshape(8, 128, 3096).transpose(1, 0, 2)
    fq, fk, fv = w3[:, :, 0:512], w3[:, :, 512:1024], w3[:, :, 1024:1536]
    ff = w3[:, :, 1536:1544]
    gq, gk, gv = w3[:, :, 1544:1800], w3[:, :, 1800:2056], w3[:, :, 2056:2568]
    glow, gout = w3[:, :, 2568:2584], w3[:, :, 2584:3096]
    wfox = np.stack([np.concatenate([fq[:, :, h * 64:(h + 1) * 64], fk[:, :, h * 64:(h + 1) * 64],
                                     fv[:, :, (h // 2) * 128:(h // 2) * 128 + 128]], axis=2) for h in range(8)], axis=0)
    out["wfox"] = f32(wfox.reshape(8, 128, 8 * 256))
    misc = np.zeros((128, 8, 40), np.float32)
    misc[:, :, 0:16] = glow
    misc[:, :, 32:40] = ff
    out["wmisc"] = f32(misc.reshape(128, 8 * 40))
    wgla = np.stack([gq, gk, gv[:, :, 0:256], gv[:, :, 256:512], gout[:, :, 0:256], gout[:, :, 256:512]], axis=0)
    out["wgla"] = f32(wgla.reshape(6, 128, 8 * 256))
    out["fb"] = f32(np.asarray(inp["fox_forget_bias"])[0].reshape(8, 1))
    out["wg2a"] = f32(np.concatenate([np.asarray(inp["gla_w_gate_up"])[0],
                                      np.asarray(inp["gla_gate_bias"])[0][None, :]], axis=0))
    out["gon"] = f32(np.asarray(inp["gla_out_norm"])[0].reshape(128, 1))
    wout = np.asarray(inp["w_out"])[0]
    out["wout"] = f32(wout.reshape(8, 128, D).transpose(1, 0, 2).reshape(128, 8 * D))
    return out


_NC_CACHE = {}


def kernel(**inputs):
    x = np.asarray(inputs["x"], dtype=np.float32)
    w = _prep_weights(inputs)
    if "nc" not in _NC_CACHE:
        _NC_CACHE["nc"] = build_nc()
    nc = _NC_CACHE["nc"]
    in_maps = []
    for c in range(N_CORES):
        m = dict(w)
        m["x"] = np.ascontiguousarray(x[c * NB:(c + 1) * NB])
        in_maps.append(m)
    res = run_bass_kernel_spmd(nc, in_maps, core_ids=list(range(N_CORES)))
    out = np.concatenate([np.asarray(r["y"]) for r in res.results], axis=0)
    if DEBUG:
        kernel.dbg = [np.asarray(r["dbg"]) for r in res.results]
    return out.astype(np.float32)
```

```python
import contextlib
import numpy as np
import concourse.bass as bass
import concourse.mybir as mybir
from concourse.bass_utils import run_bass_kernel_spmd

F32 = mybir.dt.float32
BF16 = mybir.dt.bfloat16
AF = mybir.ActivationFunctionType
ALU = mybir.AluOpType

N_CORES = 8
D = 1024
S = 2048
NB = 2
DFF = 2816
NFF = 22
TT = 16
EPS = 1e-6
GROUPS = [list(range(0, 6)), list(range(6, 12)), list(range(12, 17)), list(range(17, 22))]

DEBUG = False
STOP_AFTER = None
MIX_STOP = None
GLA_STAGES = None
B2_VAR = 0

ENGINES = ("pe", "act", "dve", "pool", "sp")


class Op:
    __slots__ = ("eng", "fn", "reads", "writes", "dma_key", "stream", "idx",
                 "deps", "waits", "signal", "count", "clock", "gidx")

    def __init__(self, eng, fn, reads, writes, dma_key):
        self.eng = eng
        self.fn = fn
        self.reads = reads
        self.writes = writes
        self.dma_key = dma_key
        self.stream = ("dma:" + str(dma_key)) if dma_key is not None else eng
        self.idx = -1
        self.deps = []
        self.waits = []
        self.signal = False
        self.count = 0
        self.clock = None


class Prog:
    def __init__(self, nc):
        self.nc = nc
        self.ops = []

    def op(self, eng, fn, reads=(), writes=(), dma_key=None):
        o = Op(eng, fn, tuple(reads), tuple(writes), dma_key)
        o.gidx = len(self.ops)
        self.ops.append(o)
        return o

    def pe(self, fn, reads=(), writes=()):
        return self.op("pe", fn, reads, writes)

    def act(self, fn, reads=(), writes=()):
        return self.op("act", fn, reads, writes)

    def dve(self, fn, reads=(), writes=()):
        return self.op("dve", fn, reads, writes)

    def pool(self, fn, reads=(), writes=()):
        return self.op("pool", fn, reads, writes)

    def dma(self, eng, key, fn, reads=(), writes=()):
        return self.op(eng, fn, reads, writes, dma_key=key)

    def finalize(self, stack):
        nc = self.nc
        ops = self.ops
        stream_ops = {}
        for o in ops:
            lst = stream_ops.setdefault(o.stream, [])
            o.idx = len(lst)
            lst.append(o)
        last_writer = {}
        readers = {}
        for o in ops:
            deps = {}
            for r in o.reads:
                w = last_writer.get(r)
                if w is not None:
                    deps[id(w)] = w
            for r in o.writes:
                w = last_writer.get(r)
                if w is not None:
                    deps[id(w)] = w
                for rd in readers.get(r, ()):
                    deps[id(rd)] = rd
            for r in o.reads:
                readers.setdefault(r, []).append(o)
            for r in o.writes:
                last_writer[r] = o
                readers[r] = []
            deps.pop(id(o), None)
            o.deps = list(deps.values())
        seen = {e: {} for e in ENGINES}
        for o in ops:
            if o.dma_key is not None:
                o.signal = True
            s = seen[o.eng]
            for d in sorted(o.deps, key=lambda d: -d.gidx):
                if d.stream == "pe" and o.eng == "pe" and o.dma_key is None:
                    continue
                if s.get(d.stream, -1) >= d.idx:
                    continue
                o.waits.append(d)
                d.signal = True
                for k, v in d.clock.items():
                    if s.get(k, -1) < v:
                        s[k] = v
            clk = dict(s)
            clk[o.stream] = o.idx
            o.clock = clk
            if o.dma_key is None and o.eng == "pe":
                s[o.stream] = o.idx
        for st, lst in stream_ops.items():
            c = 0
            for o in lst:
                if o.signal:
                    c += 1
                o.count = c
        sems = {}
        n = 0
        for st, lst in stream_ops.items():
            if any(o.signal for o in lst):
                sems[st] = stack.enter_context(nc.semaphore("sem%d" % n))
                n += 1
        self.n_sems = n
        per_eng = {e: [o for o in ops if o.eng == e] for e in ENGINES}
        block = stack.enter_context(nc.Block())

        def emit(engine, lst):
            for o in lst:
                for d in o.waits:
                    mult = 16 if d.dma_key is not None else 1
                    engine.wait_ge(sems[d.stream], d.count * mult)
                ins = o.fn(engine)
                if o.signal:
                    assert ins is not None, "signalling op must return its instruction"
                    ins.then_inc(sems[o.stream], 16 if o.dma_key is not None else 1)

        @block.sync
        def _(e):
            emit(e, per_eng["sp"])

        @block.tensor
        def _(e):
            emit(e, per_eng["pe"])

        @block.scalar
        def _(e):
            emit(e, per_eng["act"])

        @block.vector
        def _(e):
            emit(e, per_eng["dve"])

        @block.gpsimd
        def _(e):
            emit(e, per_eng["pool"])


def MM(out, pairs, start=True, stop=True):
    pairs = list(pairs)

    def fn(e):
        ins = None
        n = len(pairs)
        for i, (l, r) in enumerate(pairs):
            ins = e.matmul(out, lhsT=l, rhs=r, start=(start and i == 0), stop=(stop and i == n - 1))
        return ins
    return fn


def MMS(items):
    items = [(o, list(p)) for o, p in items]

    def fn(e):
        ins = None
        for out, pairs in items:
            n = len(pairs)
            for i, (l, r) in enumerate(pairs):
                ins = e.matmul(out, lhsT=l, rhs=r, start=(i == 0), stop=(i == n - 1))
        return ins
    return fn


def TRS(items, ident):
    items = list(items)

    def fn(e):
        ins = None
        for out, in_ in items:
            ins = e.transpose(out=out, in_=in_, identity=ident)
        return ins
    return fn


def ACTF(out, in_, func, bias=None, scale=None, accum_out=None):
    def fn(e):
        kw = {}
        if bias is not None:
            kw["bias"] = bias
        if scale is not None:
            kw["scale"] = scale
        if accum_out is not None:
            kw["accum_out"] = accum_out
        return e.activation(out=out, in_=in_, func=func, **kw)
    return fn


def COPY(out, in_):
    return lambda e: e.tensor_copy(out=out, in_=in_)


def ACOPY(out, in_):
    return lambda e: e.copy(out=out, in_=in_)


def AMUL(out, in_, m):
    return lambda e: e.mul(out=out, in_=in_, mul=m)


def TT_(out, in0, in1, op):
    return lambda e: e.tensor_tensor(out=out, in0=in0, in1=in1, op=op)


def TS(out, in0, s1, s2, op0, op1=None):
    if op1 is None:
        return lambda e: e.tensor_single_scalar(out=out, in_=in0, scalar=s1, op=op0)
    return lambda e: e.tensor_scalar(out=out, in0=in0, scalar1=s1, scalar2=s2, op0=op0, op1=op1)


def STT(out, in0, scalar, in1, op0, op1):
    return lambda e: e.scalar_tensor_tensor(out=out, in0=in0, scalar=scalar, in1=in1, op0=op0, op1=op1)


def MEMSET(ap, v):
    return lambda e: e.memset(ap, v)


def DMA(out, in_):
    return lambda e: e.dma_start(out=out, in_=in_)


def blk(slot, c0, n):
    return [(slot, b) for b in range(c0 // 512, (c0 + n - 1) // 512 + 1)]


def build_nc():
    nc = bass.Bass("TRN2", target_bir_lowering=False)
    dt = lambda name, shape, dtype=F32, kind="ExternalInput": nc.dram_tensor(name, shape, dtype, kind=kind).ap()
    x_d = dt("x", [NB, S, D])
    gains_d = dt("gains", [4, 128, D])
    wgu_d = [dt("wgu%d" % f, [NFF, 128, 2 * 8 * 128]) for f in range(2)]
    wd_d = [dt("wd%d" % f, [NFF, 128, D]) for f in range(2)]
    wfox_d = dt("wfox", [8, 128, 8 * 256])
    wmisc_d = dt("wmisc", [128, 8 * 40])
    wgla_d = dt("wgla", [6, 128, 8 * 256])
    fb_d = dt("fb", [8, 1])
    wg2a_d = dt("wg2a", [17, 256])
    gon_d = dt("gon", [128, 1])
    wout_d = dt("wout", [128, 8 * D])
    y_d = dt("y", [NB, S, D], kind="ExternalOutput")
    if DEBUG:
        dbg_d = dt("dbg", [8, 128, 16 * D], kind="ExternalOutput")

    with contextlib.ExitStack() as st:
        sb = lambda name, shape, dtype: st.enter_context(nc.sbuf_tensor(name, shape, dtype))
        x_sb = sb("x_sb", [128, TT, D], F32)
        hT = sb("hT", [128, 8, S], BF16)
        gt = sb("gt", [128, D], F32)
        hbf = sb("hbf", [128, 2, D], BF16)
        junk = sb("junk", [128, D], BF16)
        stats = sb("stats", [128, 64], F32)
        ident = sb("ident", [128, 128], BF16)
        ones_bf = sb("ones_bf", [128, 128], BF16)
        mask01 = sb("mask01", [128, 128], BF16)
        tri = sb("tri", [128, 128], BF16)
        ind = sb("ind", [128, 2], BF16)
        ones_f = sb("ones_f", [128, 64], F32)
        cst = sb("cst", [128, 4], F32)
        scan1 = sb("scan1", [8, 512], F32)
        fb = sb("fbt", [8, 1], F32)
        gon = sb("gont", [128, 1], F32)
        wg2a = sb("wg2at", [32, 256], BF16)
        wmisc = sb("wmisct", [128, 8, 40], BF16)
        pieces = sb("pieces", [72, S], BF16)
        fcar = sb("fcar", [8, 2], F32)
        decs = sb("decs", [128, TT * 4], F32)
        A = sb("slotA", [128, 24576], BF16)
        B = sb("slotB", [128, 14336], BF16)
        C = sb("slotC", [128, 8192], BF16)
        banks = [st.enter_context(nc.psum_tensor("bank%d" % i, [128, 512], F32)) for i in range(8)]
        PS = lambda b: ("ps", b)

        eps_ap = cst[:, 0:1]
        one_ap = cst[:, 1:2]

        P = Prog(nc)

        P.pool(MEMSET(ident[:], 0.0), writes=["ident"])
        P.pool(lambda e: e.affine_select(out=ident[:], in_=ident[:], pattern=[[-1, 128]], compare_op=ALU.not_equal,
                                         fill=1.0, base=0, channel_multiplier=1), writes=["ident"])
        P.pool(MEMSET(ones_bf[:], 1.0), writes=["ones_bf"])
        P.pool(MEMSET(ones_f[:], 1.0), writes=["ones_f"])
        P.pool(MEMSET(mask01[:], 1.0), writes=["mask01"])
        P.pool(lambda e: e.affine_select(out=mask01[:], in_=mask01[:], pattern=[[1, 128]], compare_op=ALU.is_ge,
                                         fill=0.0, base=0, channel_multiplier=-1), writes=["mask01"])
        P.pool(MEMSET(tri[:], 1.0 / 16.0), writes=["tri"])
        P.pool(lambda e: e.affine_select(out=tri[:], in_=tri[:], pattern=[[-1, 128]], compare_op=ALU.is_gt,
                                         fill=0.0, base=0, channel_multiplier=1), writes=["tri"])
        P.pool(MEMSET(tri[64:128, 0:64], 0.0), writes=["tri"])
        P.pool(MEMSET(ind[:], 0.0), writes=["ind"])
        P.pool(MEMSET(ind[0:64, 0:1], 1.0 / 16.0), writes=["ind"])
        P.pool(MEMSET(ind[64:128, 1:2], 1.0 / 16.0), writes=["ind"])
        P.pool(MEMSET(cst[:, 0:1], EPS), writes=["cst"])
        P.pool(MEMSET(cst[:, 1:2], 1.0), writes=["cst"])
        P.pool(MEMSET(scan1[:], 1.0), writes=["scan1"])
        P.pool(MEMSET(wg2a[:], 0.0), writes=["wg2a"])
        P.dma("sp", "c_fb", DMA(fb[:], fb_d[:, :]), writes=["fb"])
        P.dma("sp", "c_gon", DMA(gon[:], gon_d[:, :]), writes=["gon"])
        P.dma("pool", "c_wg2a", DMA(wg2a[0:17, :], wg2a_d[:, :]), reads=["wg2a"], writes=["wg2a"])
        P.dma("pool", "c_wmisc", DMA(wmisc[:].rearrange("p k c -> p (k c)"), wmisc_d[:, :]), writes=["wmisc"])

        dbg_n = [0]

        def dbg_dump(ap_src, ncols, reads):
            if not DEBUG:
                return
            i = dbg_n[0]
            dbg_n[0] += 1
            P.dma("sp", "dbg%d" % i, DMA(dbg_d[i, :, 0:ncols], ap_src), reads=reads, writes=[("dbg", i)])

        ssq = stats[:, 0:16]
        lnv = stats[:, 16:32]
        rstd = stats[:, 32:48]

        def norm_begin(gain_idx):
            P.dma("sp", "gain", DMA(gt[:], gains_d[gain_idx]), writes=["gt"])
            P.dve(MEMSET(ssq, 0.0), writes=[("ssq", g) for g in range(4)])

        def norm_stats(tg):
            for i in range(4 * tg, 4 * tg + 4):
                P.act(ACTF(junk[:], x_sb[:, i, :], AF.Square, accum_out=ssq[:, i:i + 1]),
                      reads=[("x", i, 0), ("x", i, 1)], writes=["junk", ("ssq", tg)])
            c = slice(4 * tg, 4 * tg + 4)
            P.act(ACTF(lnv[:, c], ssq[:, c], AF.Ln, bias=eps_ap, scale=1.0 / D), reads=[("ssq", tg), "cst"],
                  writes=[("lnv", tg)])
            P.act(ACTF(rstd[:, c], lnv[:, c], AF.Exp, scale=-0.5), reads=[("lnv", tg)], writes=[("rstd", tg)])

        def norm_apply(s, tg, final):
            for i in range(4 * tg, 4 * tg + 4):
                if final:
                    ob = C[:, 4096 + (i % 2) * 2048: 4096 + (i % 2 + 1) * 2048].bitcast(F32)
                    okeys = blk("C", 4096 + (i % 2) * 2048, 2048)
                    P.dve(STT(ob, x_sb[:, i, :], rstd[:, i:i + 1], gt[:], ALU.mult, ALU.mult),
                          reads=[("x", i, 0), ("x", i, 1), ("rstd", tg), "gt"], writes=okeys)
                    P.dma("sp", ("y", i % 2), DMA(y_d[s, i * 128:(i + 1) * 128, :], ob), reads=okeys,
                          writes=[("y", s, i)])
                    continue
                hb = hbf[:, i % 2, :]
                P.dve(STT(hb, x_sb[:, i, :], rstd[:, i:i + 1], gt[:], ALU.mult, ALU.mult),
                      reads=[("x", i, 0), ("x", i, 1), ("rstd", tg), "gt"], writes=[("hbf", i % 2)])
                bk = 6 + (i % 2)
                pT = banks[bk][:].bitcast(BF16).rearrange("p (k t) -> p k t", k=8)
                P.pe(TRS([(pT[:, k, :], hb[:, k * 128:(k + 1) * 128]) for k in range(8)], ident[:]),
                     reads=[("hbf", i % 2), "ident"], writes=[PS(bk)])
                P.act(ACOPY(hT[:, :, i * 128:(i + 1) * 128], pT), writes=[PS(bk), ("hT", i)])

        def make_norm_hook(s, final):
            def hook(i):
                if i % 4 != 3:
                    return
                tg = i // 4
                norm_stats(tg)
                if final:
                    norm_apply(s, tg, True)
                    return
                if tg >= 1:
                    norm_apply(s, tg - 1, final)
                if tg == 3:
                    norm_apply(s, 3, final)
            return hook

        def norm_phase_load(s, gain_idx):
            norm_begin(gain_idx)
            hook = make_norm_hook(s, False)
            for i in range(TT):
                P.dma("pool" if s > 0 else "sp", ("x", i), DMA(x_sb[:, i, :], x_d[s, i * 128:(i + 1) * 128, :]),
                      writes=[("x", i, 0), ("x", i, 1)])
                hook(i)

        def ffn_phase(f, hook=None, split_first=False, split_last=False):
            wcnt = [0]
            for gi, chunks in enumerate(GROUPS):
                abuf = gi % 2
                a0 = abuf * 12288
                n = len(chunks)
                split = (split_first and gi == 0) or (split_last and gi == len(GROUPS) - 1)
                halves = [(0, 1), (2, 3)] if split else [(0, 1, 2, 3)]
                for hi, tgs in enumerate(halves):
                    for cl, c in enumerate(chunks):
                        wb = wcnt[0] % 3
                        wcnt[0] += 1
                        wv = C[:, wb * 2048:(wb + 1) * 2048]
                        wkeys = blk("C", wb * 2048, 2048)
                        P.dma("pool", ("wgu", wb), DMA(wv, wgu_d[f][c]), writes=wkeys)
                        w4 = wv.rearrange("p (g k f) -> p g k f", g=2, k=8)
                        if hi == 0:
                            dv = B[:, abuf * 6144 + cl * 1024: abuf * 6144 + (cl + 1) * 1024]
                            dkeys = blk("B", abuf * 6144 + cl * 1024, 1024)
                            P.dma("pool", ("wd", abuf, cl), DMA(dv, wd_d[f][c]), writes=dkeys)
                        for tg in tgs:
                            par = tg % 2
                            hk = [("hT", 4 * tg + j) for j in range(4)]
                            P.pe(MM(banks[par][:], [(w4[:, 0, k, :], hT[:, k, tg * 512:(tg + 1) * 512]) for k in range(8)]),
                                 reads=wkeys + hk, writes=[PS(par)])
                            P.pe(MM(banks[2 + par][:], [(w4[:, 1, k, :], hT[:, k, tg * 512:(tg + 1) * 512]) for k in range(8)]),
                                 reads=wkeys + hk, writes=[PS(2 + par)])
                            sgc = 6144 + par * 512
                            sg = C[:, sgc:sgc + 512]
                            P.act(ACTF(sg, banks[par][:], AF.Silu), writes=[PS(par)] + blk("C", sgc, 512))
                            ac = a0 + cl * 2048 + tg * 512
                            P.dve(TT_(A[:, ac:ac + 512], banks[2 + par][:], sg, ALU.mult),
                                  reads=blk("C", sgc, 512), writes=[PS(2 + par)] + blk("A", ac, 512))
                    for i in range(4 * tgs[0], 4 * tgs[-1] + 4):
                        for dh in range(2):
                            bk = 4 + ((2 * i + dh) % 2)
                            pairs = []
                            rk = []
                            for cl in range(n):
                                ac = a0 + cl * 2048 + i * 128
                                dc = abuf * 6144 + cl * 1024 + dh * 512
                                pairs.append((A[:, ac:ac + 128], B[:, dc:dc + 512]))
                                rk += blk("A", ac, 128) + blk("B", dc, 512)
                            P.pe(MM(banks[bk][:], pairs), reads=rk, writes=[PS(bk)])
                            xv = x_sb[:, i, dh * 512:(dh + 1) * 512]
                            P.dve(STT(xv, banks[bk][:], 0.5, xv, ALU.mult, ALU.add), writes=[PS(bk), ("x", i, dh)])
                        if hook is not None and gi == len(GROUPS) - 1:
                            hook(i)

        def mix_phase(s, hook=None):
            concat0 = 0
            GK0 = 0
            GQ0 = 4096
            GV0 = 16384
            WO0 = 16384

            def load_wchunk(b, src, ncols):
                wv = C[:, b * 2048:b * 2048 + ncols]
                keys = blk("C", b * 2048, 2048)
                P.dma("pool", ("wchunk", b), DMA(wv, src), writes=keys)
                return wv.rearrange("p (k c) -> p k c", k=8), keys

            wc = [0]

            def next_wchunk(src, ncols=2048):
                b = wc[0] % 2
                wc[0] += 1
                return load_wchunk(b, src, ncols)

            for half in range(2):
                w3, wk = next_wchunk(wgla_d[4 + half])
                for hh in range(2):
                    h = 2 * half + hh
                    for tg in range(4):
                        bk = (hh * 4 + tg) % 2
                        hk = [("hT", 4 * tg + j) for j in range(4)]
                        P.pe(MM(banks[bk][:], [(w3[:, k, hh * 128:(hh + 1) * 128], hT[:, k, tg * 512:(tg + 1) * 512])
                                               for k in range(8)]), reads=wk + hk, writes=[PS(bk)])
                        oc = concat0 + (4 + h) * 2048 + tg * 512
                        P.act(ACTF(A[:, oc:oc + 512], banks[bk][:], AF.Silu), writes=[PS(bk)] + blk("A", oc, 512))
                        P.dve(TS(A[:, oc:oc + 512], A[:, oc:oc + 512], gon[:, 0:1], None, ALU.mult), reads=["gon"],
                              writes=blk("A", oc, 512))
            w3, wk = next_wchunk(wgla_d[1])
            for i in range(TT):
                bk = i % 2
                P.pe(MM(banks[bk][:, 0:256], [(hT[:, k, i * 128:(i + 1) * 128], w3[:, k, :]) for k in range(8)]),
                     reads=wk + [("hT", i)], writes=[PS(bk)])
                oc = GK0 + i * 256
                P.dve(COPY(A[:, oc:oc + 256], banks[bk][:, 0:256]), writes=[PS(bk)] + blk("A", oc, 256))
            def proj_units():
                w3, wk = next_wchunk(wgla_d[0])
                for j in range(2):
                    for tg in range(4):
                        def u(j=j, tg=tg, w3=w3, wk=wk):
                            bk = 3 + ((j * 4 + tg) % 2)
                            hk = [("hT", 4 * tg + jj) for jj in range(4)]
                            P.pe(MM(banks[bk][:], [(w3[:, k, j * 128:(j + 1) * 128], hT[:, k, tg * 512:(tg + 1) * 512])
                                                   for k in range(8)]), reads=wk + hk, writes=[PS(bk)])
                            oc = GQ0 + j * 2048 + tg * 512
                            P.act(AMUL(A[:, oc:oc + 512], banks[bk][:], 0.125), writes=[PS(bk)] + blk("A", oc, 512))
                        yield u
                for half in range(2):
                    w3, wk = next_wchunk(wgla_d[2 + half])
                    for i in range(TT):
                        def u(half=half, i=i, w3=w3, wk=wk):
                            bk = 3 + (i % 2)
                            P.pe(MM(banks[bk][:, 0:256], [(hT[:, k, i * 128:(i + 1) * 128], w3[:, k, :]) for k in range(8)]),
                                 reads=wk + [("hT", i)], writes=[PS(bk)])
                            oc = GV0 + i * 512 + half * 256
                            P.act(ACOPY(A[:, oc:oc + 256], banks[bk][:, 0:256]), writes=[PS(bk)] + blk("A", oc, 256))
                        yield u

            if MIX_STOP == 1:
                return
            def Bf(c0, n):
                return B[:, 2 * c0:2 * (c0 + n)].bitcast(F32), blk("B", 2 * c0, 2 * n)

            def Bv(c0, n, dtype=BF16):
                v = B[:, c0:c0 + n]
                return (v.bitcast(F32) if dtype == F32 else v), blk("B", c0, n)

            def a_v(par): return Bv(par * 512, 512, F32)
            def lsg_v(par): return Bv(1024 + par * 256, 256)
            def eR_v(par): return Bv(1536 + par * 544, 544, F32)
            def S_v(par): return Bv(3072 + par * 1024, 1024, F32)
            def Sbf_v(par): return Bv(5120 + par * 512, 512)
            def osq_v(par): return Bv(6144 + par * 512, 512)
            rs_v = Bv(7168, 1024, F32)
            t1_v = Bv(8192, 1024, F32)
            def glT_v(par): return Bv(9216 + par * 512, 512)
            def Cf(c0, n):
                return C[:, 4096 + 2 * c0: 4096 + 2 * (c0 + n)].bitcast(F32), blk("C", 4096 + 2 * c0, 2 * n)
            fz_v = Cf(0, 512)
            fa_v = Cf(512, 512)
            fc_v = Cf(1024, 512)
            fr_v = Cf(1536, 512)

            bmask = junk[:, 0:512]
            P.pool(MEMSET(bmask, 0.0), writes=["junk"])
            for j in range(2):
                P.pool(MEMSET(junk[0:64, j * 256:j * 256 + 128], 1.0), writes=["junk"])
                P.pool(MEMSET(junk[64:128, j * 256 + 128:j * 256 + 256], 1.0), writes=["junk"])
            Sinit, Sinit_k = S_v(1)
            P.dve(MEMSET(Sinit, 0.0), writes=Sinit_k)

            def pro_head(tg):
                par = tg % 2
                hk = [("hT", 4 * tg + j) for j in range(4)]
                P.pe(MM(banks[1][0:40, :], [(wmisc[:, k, :], hT[:, k, tg * 512:(tg + 1) * 512]) for k in range(8)]),
                     reads=["wmisc"] + hk, writes=[PS(1)])
                g, gk_ = glT_v(par)
                P.dve(MEMSET(g[0:32, :], 1.0), writes=gk_)
                P.dve(COPY(g[0:16, :], banks[1][0:16, :]), writes=[PS(1)] + gk_)
                fz, fzk = fz_v
                P.dve(TS(fz[0:8, :], banks[1][32:40, :], fb[0:8, 0:1], None, ALU.add), reads=["fb"],
                      writes=[PS(1)] + fzk)

            def pro_tail_ops(tg):
                fz, fzk = fz_v
                fa, fak = fa_v
                fc, fck = fc_v
                fr, frk = fr_v
                tr = slice(tg * 512, (tg + 1) * 512)
                pk = [("pieces", tg)]
                midt = C[:, 5120:5632]
                ops = []
                ops.append(lambda: P.dve(STT(fa[0:8, :], fz[0:8, :], -1.0, fz[0:8, :], ALU.mult, ALU.max), reads=fzk, writes=fak))
                ops.append(lambda: P.act(ACTF(fa[0:8, :], fa[0:8, :], AF.Exp, scale=-1.0), writes=fak))
                ops.append(lambda: P.act(ACTF(fa[0:8, :], fa[0:8, :], AF.Ln, bias=one_ap[0:8, :]), reads=["cst"], writes=fak))
                ops.append(lambda: P.dve(STT(fz[0:8, :], fz[0:8, :], 0.0, fa[0:8, :], ALU.min, ALU.subtract), reads=fak, writes=fzk))
                if tg == 0:
                    ops.append(lambda: P.dve(lambda e: e.tensor_tensor_scan(out=fc[0:8, :], data0=scan1[:], data1=fz[0:8, :],
                                                                            initial=0.0, op0=ALU.mult, op1=ALU.add),
                                             reads=fzk + ["scan1"], writes=fck))
                else:
                    ops.append(lambda: P.dve(lambda e: e.tensor_tensor_scan(out=fc[0:8, :], data0=scan1[:], data1=fz[0:8, :],
                                                                            initial=fcar[0:8, 0:1], op0=ALU.mult, op1=ALU.add),
                                             reads=fzk + ["scan1", "fcar"], writes=fck))
                ops.append(lambda: P.dve(COPY(fcar[0:8, 0:1], fc[0:8, 511:512]), reads=fck, writes=["fcar"]))
                ops.append(lambda: P.dve(COPY(pieces[0:8, tr], fc[0:8, :]), reads=fck, writes=pk))
                ops.append(lambda: P.dve(TT_(fr[0:8, :], fc[0:8, :], pieces[0:8, tr], ALU.subtract), reads=fck + pk, writes=frk))
                ops.append(lambda: P.dve(COPY(midt[0:8, :], fr[0:8, :]), reads=frk, writes=fak))
                ops.append(lambda: P.dve(COPY(pieces[32:40, tr], midt[0:8, :]), reads=fak, writes=pk))
                ops.append(lambda: P.dve(TT_(fc[0:8, :], fr[0:8, :], midt[0:8, :], ALU.subtract), reads=frk + fak, writes=fck))
                ops.append(lambda: P.dve(COPY(pieces[64:72, tr], fc[0:8, :]), reads=fck, writes=pk))
                return ops

            def stage_A1(t):
                par = t % 2
                tg = t // 4
                g, gk_ = glT_v(tg % 2)
                c0 = (t % 4) * 128
                P.pe(MM(banks[0][:, 0:256], [(g[0:17, c0:c0 + 128], wg2a[0:17, :])]),
                     reads=gk_ + ["wg2a"], writes=[PS(0)])
                a, ak = a_v(par)
                l, lk = lsg_v(par)
                P.act(ACTF(a, banks[0][:, 0:256], AF.Abs), writes=[PS(0)] + ak)
                P.act(ACTF(a, a, AF.Exp, scale=-1.0), writes=ak)
                P.act(ACTF(a, a, AF.Ln, bias=one_ap), reads=["cst"], writes=ak)
                P.dve(STT(l, banks[0][:, 0:256], 0.0, a, ALU.min, ALU.subtract), reads=ak, writes=[PS(0)] + lk)

            def stage_A2(t):
                par = t % 2
                l, lk = lsg_v(par)
                e_, ek = eR_v(par)
                P.pe(MMS([(banks[2][:, 0:256], [(tri[:], l)]),
                          (banks[2][:, 256:258], [(l[:, 0:128], ind[:])]),
                          (banks[2][:, 258:260], [(l[:, 128:256], ind[:])])]),
                     reads=lk + ["tri", "ind"], writes=[PS(2)])
                P.act(ACTF(e_[:, 0:256], banks[2][:, 0:256], AF.Exp), writes=[PS(2)] + ek)
                P.act(ACTF(decs[:, 4 * t:4 * t + 4], banks[2][:, 256:260], AF.Exp), writes=[PS(2), ("decs", t)])
                kc = GK0 + t * 256
                P.pool(TT_(A[:, kc:kc + 256], A[:, kc:kc + 256], e_[:, 0:256], ALU.mult), reads=ek, writes=blk("A", kc, 256))

            def stage_B1(t):
                kc = GK0 + t * 256
                vc = GV0 + t * 512
                for c in range(2):
                    par = c
                    pb_ = 3 + par
                    rows = slice(c * 64, (c + 1) * 64)
                    P.pe(MMS([(banks[pb_][:, j * 256:(j + 1) * 256],
                               [(A[rows, kc + j * 128: kc + (j + 1) * 128], A[rows, vc + j * 256: vc + (j + 1) * 256])])
                              for j in range(2)]),
                         reads=blk("A", kc, 256) + blk("A", vc, 512), writes=[PS(pb_)])
                for c in range(2):
                    par = c
                    pb_ = 3 + par
                    Sn, Snk = S_v(par)
                    Sp, Spk = S_v(1 - par)
                    for j in range(2):
                        P.dve(STT(Sn[:, j * 256:(j + 1) * 256], Sp[:, j * 256:(j + 1) * 256],
                                  decs[:, 4 * t + 2 * j + c: 4 * t + 2 * j + c + 1],
                                  banks[pb_][:, j * 256:(j + 1) * 256], ALU.mult, ALU.add),
                              reads=Spk + [("decs", t)], writes=[PS(pb_)] + Snk)
                    sb_, sbk = Sbf_v(par)
                    P.dve(TT_(sb_, Sn, bmask, ALU.mult), reads=Snk + ["junk"], writes=sbk)

            def po_bank(t):
                return 5 if t % 2 == 0 else 7

            def stage_B2(t):
                items = []
                rk = []
                bk = po_bank(t)
                for c in range(2):
                    sb_, sbk = Sbf_v(c)
                    rk += sbk
                    for h in range(4):
                        j, hh = h // 2, h % 2
                        lhsT = sb_[:, j * 256 + hh * 128: j * 256 + (hh + 1) * 128]
                        qc = GQ0 + j * 2048 + t * 128 + c * 64
                        rk += blk("A", qc, 64)
                        items.append((banks[bk][:, h * 128 + c * 64: h * 128 + (c + 1) * 64], [(lhsT, A[:, qc:qc + 64])]))
                P.pe(MMS(items), reads=rk, writes=[PS(bk)])

            def stage_B2b(t):
                bk = po_bank(t)
                osq, ok_ = osq_v(t % 2)
                P.act(ACTF(osq, banks[bk][:], AF.Square), writes=[PS(bk)] + ok_)

            def stage_C(t):
                bk = po_bank(t)
                osq, ok_ = osq_v(t % 2)
                rs, rsk = rs_v
                t1, t1k = t1_v
                P.pe(MM(banks[6][:], [(ones_bf[:], osq)]), reads=ok_ + ["ones_bf"], writes=[PS(6)])
                P.act(ACTF(rs, banks[6][:], AF.Ln, bias=eps_ap, scale=1.0 / 128.0), reads=["cst"], writes=[PS(6)] + rsk)
                P.act(ACTF(rs, rs, AF.Exp, scale=-0.5), writes=rsk)
                P.dve(TT_(t1, banks[bk][:], rs, ALU.mult), reads=rsk, writes=[PS(bk)] + t1k)
                gv3 = A[:, concat0 + 4 * 2048: concat0 + 8 * 2048].rearrange("p (h t) -> p h t", h=4)[:, :, t * 128:(t + 1) * 128]
                gkeys = [("A", (concat0 + (4 + h) * 2048 + t * 128) // 512) for h in range(4)]
                P.pool(TT_(gv3, t1.rearrange("p (h t) -> p h t", h=4), gv3, ALU.mult), reads=t1k, writes=gkeys)

            fox_w = {}

            def fox_load(h):
                fox_w[h] = next_wchunk(wfox_d[h], 2048)

            units_it = proj_units()
            n_units = 8 + 2 * TT
            emitted = 0
            pending = []
            for step in range(TT + 1):
                t = step
                if t < TT and t % 4 == 0:
                    pro_head(t // 4)
                    pending.extend(pro_tail_ops(t // 4))
                if t < TT:
                    stage_A1(t)
                if 0 <= step - 1 < TT:
                    stage_A2(step - 1)
                want = min(n_units, ((step + 1) * n_units + TT) // (TT + 1))
                while emitted < want:
                    next(units_it)()
                    emitted += 1
                for _ in range(4):
                    if pending:
                        pending.pop(0)()
            while emitted < n_units:
                next(units_it)()
                emitted += 1
            while pending:
                pending.pop(0)()
            fox_load(0)
            fox_load(1)
            for step in range(TT + 2):
                if 0 <= step - 1 < TT:
                    stage_B2(step - 1)
                if step < TT:
                    stage_B1(step)
                if 0 <= step - 1 < TT:
                    stage_B2b(step - 1)
                if 0 <= step - 2 < TT:
                    stage_C(step - 2)
            if MIX_STOP == 2:
                return
            for b in range(2):
                q0, k0 = b * 2048, 4096 + b * 2048
                P.pool(MEMSET(B[64:70, q0:q0 + 2048], -1.0),
                       writes=blk("B", q0, 2048) + [("auginit", b)] + [("augq", b, r) for r in range(3)])
                P.pool(MEMSET(B[64:70, k0:k0 + 2048], 1.0),
                       writes=blk("B", k0, 2048) + [("auginit", b)] + [("augk", b, r) for r in range(3)])
                v0 = 8192 + b * 3072
                P.dve(MEMSET(B[:, v0:v0 + 3072].rearrange("p (i c) -> p i c", c=192)[:, :, 64:128], 1.0),
                      writes=blk("B", v0, 3072))

            def proj_qk(h, tg):
                b = h % 2
                q0, k0 = b * 2048, 4096 + b * 2048
                w3, wk = fox_w[h]
                hk = [("hT", 4 * tg + j) for j in range(4)]
                P.pe(MM(banks[6][:], [(w3[:, k, 0:128], hT[:, k, tg * 512:(tg + 1) * 512]) for k in range(8)]),
                     reads=wk + hk, writes=[PS(6)])
                qc = q0 + tg * 512
                kc = k0 + tg * 512
                P.dve(TS(B[0:64, qc:qc + 512], banks[6][0:64, :], 0.125, None, ALU.mult), writes=[PS(6)] + blk("B", qc, 512))
                P.dve(COPY(B[0:64, kc:kc + 512], banks[6][64:128, :]), writes=[PS(6)] + blk("B", kc, 512))

            def proj_v(pair, tg):
                h = 2 * pair
                w3, wk = fox_w[h]
                v0 = 8192 + (pair % 2) * 3072
                v3 = B[:, v0:v0 + 3072].rearrange("p (i c) -> p i c", c=192)
                hk = [("hT", 4 * tg + j) for j in range(4)]
                P.pe(MMS([(banks[7][:, j * 128:(j + 1) * 128],
                           [(hT[:, k, (4 * tg + j) * 128:(4 * tg + j + 1) * 128], w3[:, k, 128:256]) for k in range(8)])
                          for j in range(4)]), reads=wk + hk, writes=[PS(7)])
                p4 = banks[7][:].rearrange("p (j e c) -> p j e c", j=4, e=2)
                vk = blk("B", v0 + 4 * tg * 192, 4 * 192)
                P.dve(COPY(v3[:, 4 * tg:4 * tg + 4, 0:64], p4[:, :, 0, :]), writes=[PS(7)] + vk)
                P.dve(COPY(v3[:, 4 * tg:4 * tg + 4, 128:192], p4[:, :, 1, :]), writes=[PS(7)] + vk)

            def proj_aug(h):
                b = h % 2
                q0, k0 = b * 2048, 4096 + b * 2048
                for r in range(3):
                    P.dma("sp", ("augq", b, r), DMA(B[67 + r:68 + r, q0:q0 + 2048], pieces[32 * r + h:32 * r + h + 1, :]),
                          reads=[("pieces", g_) for g_ in range(4)], writes=[("augq", b, r)])
                    P.dma("sp", ("augk", b, r), DMA(B[64 + r:65 + r, k0:k0 + 2048], pieces[32 * r + h:32 * r + h + 1, :]),
                          reads=[("pieces", g_) for g_ in range(4)], writes=[("augk", b, r)])

            SKEW = 3
            units = []
            for h in range(8):
                for tg in range(4):
                    nj = 4 * tg + 4
                    for j in range(nj):
                        units.append((h, tg, j, nj))
            grp_cnt = {}

            def unit_bufs(idx):
                r = idx % 4
                pr = (0, 1, 2, 5)[r]
                pT0 = 6144 + r * 512
                return pr, pT0

            def emit_qk(idx):
                h, tg, j, nj = units[idx]
                b = h % 2
                q0, k0 = b * 2048, 4096 + b * 2048
                augk = [("augq", b, r) for r in range(3)] + [("augk", b, r) for r in range(3)] + [("auginit", b)]
                t0 = tg * 512
                jj = j - 4 * tg
                off = jj * 128 if jj > 0 else 0
                W = 512 - off
                pr, pT0 = unit_bufs(idx)
                pTk = blk("C", pT0, 512)
                P.pe(MM(banks[pr][:, 0:W], [(B[0:70, k0 + j * 128: k0 + (j + 1) * 128],
                                             B[0:70, q0 + t0 + off: q0 + t0 + 512])]),
                     reads=blk("B", k0 + j * 128, 128) + blk("B", q0 + t0, 512) + augk, writes=[PS(pr)])
                P.act(ACTF(C[:, pT0:pT0 + W], banks[pr][:, 0:W], AF.Exp), writes=[PS(pr)] + pTk)
                if jj >= 0:
                    P.pool(TT_(C[:, pT0:pT0 + 128], C[:, pT0:pT0 + 128], mask01[:], ALU.mult),
                           reads=["mask01"], writes=pTk)

            def emit_pv(idx):
                h, tg, j, nj = units[idx]
                pair = h // 2
                odd = h % 2
                v0 = 8192 + (pair % 2) * 3072
                v3 = B[:, v0:v0 + 3072].rearrange("p (i c) -> p i c", c=192)
                g = (h, tg)
                if g not in grp_cnt:
                    grp_cnt[g] = len(grp_cnt)
                ob = 3 + (grp_cnt[g] % 2)
                jj = j - 4 * tg
                off = jj * 128 if jj > 0 else 0
                W = 512 - off
                pr, pT0 = unit_bufs(idx)
                pTk = blk("C", pT0, 512)
                lhsT = v3[:, j, 64:192] if odd else v3[:, j, 0:128]
                P.pe(MM(banks[ob][:, off:512], [(lhsT, C[:, pT0:pT0 + W])], start=(j == 0), stop=(j == nj - 1)),
                     reads=pTk + blk("B", v0 + j * 192, 192), writes=[PS(ob)])
                if j == nj - 1:
                    rc0 = 4096 + (grp_cnt[g] % 2) * 1024
                    rcp = C[:, rc0:rc0 + 1024].bitcast(F32)
                    rck = blk("C", rc0, 1024)
                    drow = slice(64, 128) if odd else slice(0, 64)
                    srow = slice(0, 64) if odd else slice(64, 128)
                    P.dve(lambda e, o_=rcp[drow, :], i_=banks[ob][srow, :]: e.reciprocal(out=o_, in_=i_),
                          writes=[PS(ob)] + rck)
                    oc = concat0 + (h // 2) * 2048 + tg * 512
                    P.dve(TT_(A[drow, oc:oc + 512], banks[ob][drow, :], rcp[drow, :], ALU.mult), reads=rck,
                          writes=[PS(ob)] + blk("A", oc, 512))

            if MIX_STOP == 3:
                return
            for tg in range(4):
                proj_qk(0, tg)
                proj_v(0, tg)
            proj_aug(0)
            fox_load(2)
            wo3 = A[:, WO0:WO0 + 8192].rearrange("p (k d) -> p k d", k=8)
            wok = blk("A", WO0, 8192)
            P.dma("pool", "wout", DMA(A[:, WO0:WO0 + 8192], wout_d[:, :]), writes=wok)
            if MIX_STOP == 4:
                return
            side = {}
            base = 0
            for h in range(8):
                if h + 1 < 8:
                    for tg in range(4):
                        side.setdefault(base + 4 + 8 * tg, []).append(("qk", h + 1, tg))
                    side.setdefault(base + 5, []).append(("aug", h + 1))
                    if h % 2 == 1:
                        for tg in range(4):
                            side.setdefault(base + 8 + 8 * tg, []).append(("v", (h + 1) // 2, tg))
                    if h + 3 < 8:
                        side.setdefault(base + 34, []).append(("load", h + 3))
                base += 40
            for idx in range(len(units) + SKEW):
                if idx < len(units):
                    emit_qk(idx)
                if idx - SKEW >= 0:
                    emit_pv(idx - SKEW)
                for item in side.get(idx, ()):
                    if item[0] == "qk":
                        proj_qk(item[1], item[2])
                    elif item[0] == "v":
                        proj_v(item[1], item[2])
                    elif item[0] == "aug":
                        proj_aug(item[1])
                    else:
                        fox_load(item[1])

            if MIX_STOP == 5:
                return
            cT = A[:, concat0:concat0 + 16384].rearrange("p (k t) -> p k t", k=8)
            for i in range(TT):
                for dh in range(2):
                    bk = (2 * i + dh) % 2
                    rk = list(wok)
                    for k in range(8):
                        rk += blk("A", concat0 + k * 2048 + i * 128, 128)
                    P.pe(MM(banks[bk][:], [(cT[:, k, i * 128:(i + 1) * 128], wo3[:, k, dh * 512:(dh + 1) * 512])
                                           for k in range(8)]), reads=rk, writes=[PS(bk)])
                    xv = x_sb[:, i, dh * 512:(dh + 1) * 512]
                    P.dve(TT_(xv, banks[bk][:], xv, ALU.add), writes=[PS(bk), ("x", i, dh)])
                if hook is not None:
                    hook(i)

        def dump_x(s):
            for i in range(TT):
                P.dma("sp", ("yd", i % 2), DMA(y_d[s, i * 128:(i + 1) * 128, :], x_sb[:, i, :]),
                      reads=[("x", i, 0), ("x", i, 1)], writes=[("y", s, i)])

        for s in range(NB if STOP_AFTER is None else 1):
            norm_phase_load(s, 0)
            if STOP_AFTER == 0:
                dump_x(s); continue
            if STOP_AFTER == 1:
                ffn_phase(0)
                dump_x(s); continue
            norm_begin(1)
            ffn_phase(0, make_norm_hook(s, False), split_first=True)
            if STOP_AFTER == 2:
                mix_phase(s)
                dump_x(s); continue
            norm_begin(2)
            mix_phase(s, make_norm_hook(s, False))
            if STOP_AFTER == 3:
                ffn_phase(1)
                dump_x(s); continue
            norm_begin(3)
            ffn_phase(1, make_norm_hook(s, True), split_last=True)
        P.op("sp", lambda e: None, reads=[("y", s, i) for s in range(NB if STOP_AFTER is None else 1) for i in range(TT)]
             + ([("dbg", i) for i in range(dbg_n[0])] if DEBUG else []))
        P.finalize(st)
    return nc


def _prep_weights(inp):
    f32 = lambda a: np.ascontiguousarray(np.asarray(a, dtype=np.float32))
    out = {}
    gains = np.stack([np.asarray(inp["ffn1_norm"])[0], np.asarray(inp["mix_norm"])[0],
                      np.asarray(inp["ffn2_norm"])[0], np.asarray(inp["final_norm"])], axis=0)
    out["gains"] = f32(np.broadcast_to(gains[:, None, :], (4, 128, D)))
    for f, pre in enumerate(("ffn1", "ffn2")):
        wg = np.asarray(inp[pre + "_w_gate"])[0]
        wu = np.asarray(inp[pre + "_w_up"])[0]
        wd = np.asarray(inp[pre + "_w_down"])[0]
        g = wg.reshape(8, 128, NFF, 128).transpose(2, 1, 0, 3)
        u = wu.reshape(8, 128, NFF, 128).transpose(2, 1, 0, 3)
        out["wgu%d" % f] = f32(np.stack([g, u], axis=2).reshape(NFF, 128, 2 * 8 * 128))
        out["wd%d" % f] = f32(wd.reshape(NFF, 128, D))
    win = np.asarray(inp["w_in"])[0]
    w3 = win.reshape(8, 128, 3096).transpose(1, 0, 2)
    fq, fk, fv = w3[:, :, 0:512], w3[:, :, 512:1024], w3[:, :, 1024:1536]
    ff = w3[:, :, 1536:1544]
    gq, gk, gv = w3[:, :, 1544:1800], w3[:, :, 1800:2056], w3[:, :, 2056:2568]
    glow, gout = w3[:, :, 2568:2584], w3[:, :, 2584:3096]
    wfox = np.stack([np.concatenate([fq[:, :, h * 64:(h + 1) * 64], fk[:, :, h * 64:(h + 1) * 64],
                                     fv[:, :, (h // 2) * 128:(h // 2) * 128 + 128]], axis=2) for h in range(8)], axis=0)
    out["wfox"] = f32(wfox.reshape(8, 128, 8 * 256))
    misc = np.zeros((128, 8, 40), np.float32)
    misc[:, :, 0:16] = glow
    misc[:, :, 32:40] = ff
    out["wmisc"] = f32(misc.reshape(128, 8 * 40))
    wgla = np.stack([gq, gk, gv[:, :, 0:256], gv[:, :, 256:512], gout[:, :, 0:256], gout[:, :, 256:512]], axis=0)
    out["wgla"] = f32(wgla.reshape(6, 128, 8 * 256))
    out["fb"] = f32(np.asarray(inp["fox_forget_bias"])[0].reshape(8, 1))
    out["wg2a"] = f32(np.concatenate([np.asarray(inp["gla_w_gate_up"])[0],
                                      np.asarray(inp["gla_gate_bias"])[0][None, :]], axis=0))
    out["gon"] = f32(np.asarray(inp["gla_out_norm"])[0].reshape(128, 1))
    wout = np.asarray(inp["w_out"])[0]
    out["wout"] = f32(wout.reshape(8, 128, D).transpose(1, 0, 2).reshape(128, 8 * D))
    return out


_NC_CACHE = {}


def kernel(**inputs):
    x = np.asarray(inputs["x"], dtype=np.float32)
    w = _prep_weights(inputs)
    if "nc" not in _NC_CACHE:
        _NC_CACHE["nc"] = build_nc()
    nc = _NC_CACHE["nc"]
    in_maps = []
    for c in range(N_CORES):
        m = dict(w)
        m["x"] = np.ascontiguousarray(x[c * NB:(c + 1) * NB])
        in_maps.append(m)
    res = run_bass_kernel_spmd(nc, in_maps, core_ids=list(range(N_CORES)))
    out = np.concatenate([np.asarray(r["y"]) for r in res.results], axis=0)
    if DEBUG:
        kernel.dbg = [np.asarray(r["dbg"]) for r in res.results]
    return out.astype(np.float32)
```

```python
import contextlib
import numpy as np
import concourse.bass as bass
import concourse.mybir as mybir
from concourse.bass_utils import run_bass_kernel_spmd

F32 = mybir.dt.float32
BF16 = mybir.dt.bfloat16
AF = mybir.ActivationFunctionType
ALU = mybir.AluOpType

N_CORES = 8
D = 1024
S = 2048
NB = 2
DFF = 2816
NFF = 22
TT = 16
EPS = 1e-6
GROUPS = [list(range(0, 6)), list(range(6, 12)), list(range(12, 17)), list(range(17, 22))]

DEBUG = False
STOP_AFTER = None
MIX_STOP = None
GLA_STAGES = None
B2_VAR = 0

ENGINES = ("pe", "act", "dve", "pool", "sp")


class Op:
    __slots__ = ("eng", "fn", "reads", "writes", "dma_key", "stream", "idx",
                 "deps", "waits", "signal", "count", "clock", "gidx")

    def __init__(self, eng, fn, reads, writes, dma_key):
        self.eng = eng
        self.fn = fn
        self.reads = reads
        self.writes = writes
        self.dma_key = dma_key
        self.stream = ("dma:" + str(dma_key)) if dma_key is not None else eng
        self.idx = -1
        self.deps = []
        self.waits = []
        self.signal = False
        self.count = 0
        self.clock = None


class Prog:
    def __init__(self, nc):
        self.nc = nc
        self.ops = []

    def op(self, eng, fn, reads=(), writes=(), dma_key=None):
        o = Op(eng, fn, tuple(reads), tuple(writes), dma_key)
        o.gidx = len(self.ops)
        self.ops.append(o)
        return o

    def pe(self, fn, reads=(), writes=()):
        return self.op("pe", fn, reads, writes)

    def act(self, fn, reads=(), writes=()):
        return self.op("act", fn, reads, writes)

    def dve(self, fn, reads=(), writes=()):
        return self.op("dve", fn, reads, writes)

    def pool(self, fn, reads=(), writes=()):
        return self.op("pool", fn, reads, writes)

    def dma(self, eng, key, fn, reads=(), writes=()):
        return self.op(eng, fn, reads, writes, dma_key=key)

    def finalize(self, stack):
        nc = self.nc
        ops = self.ops
        stream_ops = {}
        for o in ops:
            lst = stream_ops.setdefault(o.stream, [])
            o.idx = len(lst)
            lst.append(o)
        last_writer = {}
        readers = {}
        for o in ops:
            deps = {}
            for r in o.reads:
                w = last_writer.get(r)
                if w is not None:
                    deps[id(w)] = w
            for r in o.writes:
                w = last_writer.get(r)
                if w is not None:
                    deps[id(w)] = w
                for rd in readers.get(r, ()):
                    deps[id(rd)] = rd
            for r in o.reads:
                readers.setdefault(r, []).append(o)
            for r in o.writes:
                last_writer[r] = o
                readers[r] = []
            deps.pop(id(o), None)
            o.deps = list(deps.values())
        seen = {e: {} for e in ENGINES}
        for o in ops:
            if o.dma_key is not None:
                o.signal = True
            s = seen[o.eng]
            for d in sorted(o.deps, key=lambda d: -d.gidx):
                if d.stream == "pe" and o.eng == "pe" and o.dma_key is None:
                    continue
                if s.get(d.stream, -1) >= d.idx:
                    continue
                o.waits.append(d)
                d.signal = True
                for k, v in d.clock.items():
                    if s.get(k, -1) < v:
                        s[k] = v
            clk = dict(s)
            clk[o.stream] = o.idx
            o.clock = clk
            if o.dma_key is None and o.eng == "pe":
                s[o.stream] = o.idx
        for st, lst in stream_ops.items():
            c = 0
            for o in lst:
                if o.signal:
                    c += 1
                o.count = c
        sems = {}
        n = 0
        for st, lst in stream_ops.items():
            if any(o.signal for o in lst):
                sems[st] = stack.enter_context(nc.semaphore("sem%d" % n))
                n += 1
        self.n_sems = n
        per_eng = {e: [o for o in ops if o.eng == e] for e in ENGINES}
        block = stack.enter_context(nc.Block())

        def emit(engine, lst):
            for o in lst:
                for d in o.waits:
                    mult = 16 if d.dma_key is not None else 1
                    engine.wait_ge(sems[d.stream], d.count * mult)
                ins = o.fn(engine)
                if o.signal:
                    assert ins is not None, "signalling op must return its instruction"
                    ins.then_inc(sems[o.stream], 16 if o.dma_key is not None else 1)

        @block.sync
        def _(e):
            emit(e, per_eng["sp"])

        @block.tensor
        def _(e):
            emit(e, per_eng["pe"])

        @block.scalar
        def _(e):
            emit(e, per_eng["act"])

        @block.vector
        def _(e):
            emit(e, per_eng["dve"])

        @block.gpsimd
        def _(e):
            emit(e, per_eng["pool"])


def MM(out, pairs, start=True, stop=True):
    pairs = list(pairs)

    def fn(e):
        ins = None
        n = len(pairs)
        for i, (l, r) in enumerate(pairs):
            ins = e.matmul(out, lhsT=l, rhs=r, start=(start and i == 0), stop=(stop and i == n - 1))
        return ins
    return fn


def MMS(items):
    items = [(o, list(p)) for o, p in items]

    def fn(e):
        ins = None
        for out, pairs in items:
            n = len(pairs)
            for i, (l, r) in enumerate(pairs):
                ins = e.matmul(out, lhsT=l, rhs=r, start=(i == 0), stop=(i == n - 1))
        return ins
    return fn


def TRS(items, ident):
    items = list(items)

    def fn(e):
        ins = None
        for out, in_ in items:
            ins = e.transpose(out=out, in_=in_, identity=ident)
        return ins
    return fn


def ACTF(out, in_, func, bias=None, scale=None, accum_out=None):
    def fn(e):
        kw = {}
        if bias is not None:
            kw["bias"] = bias
        if scale is not None:
            kw["scale"] = scale
        if accum_out is not None:
            kw["accum_out"] = accum_out
        return e.activation(out=out, in_=in_, func=func, **kw)
    return fn


def COPY(out, in_):
    return lambda e: e.tensor_copy(out=out, in_=in_)


def ACOPY(out, in_):
    return lambda e: e.copy(out=out, in_=in_)


def AMUL(out, in_, m):
    return lambda e: e.mul(out=out, in_=in_, mul=m)


def TT_(out, in0, in1, op):
    return lambda e: e.tensor_tensor(out=out, in0=in0, in1=in1, op=op)


def TS(out, in0, s1, s2, op0, op1=None):
    if op1 is None:
        return lambda e: e.tensor_single_scalar(out=out, in_=in0, scalar=s1, op=op0)
    return lambda e: e.tensor_scalar(out=out, in0=in0, scalar1=s1, scalar2=s2, op0=op0, op1=op1)


def STT(out, in0, scalar, in1, op0, op1):
    return lambda e: e.scalar_tensor_tensor(out=out, in0=in0, scalar=scalar, in1=in1, op0=op0, op1=op1)


def MEMSET(ap, v):
    return lambda e: e.memset(ap, v)


def DMA(out, in_):
    return lambda e: e.dma_start(out=out, in_=in_)


def blk(slot, c0, n):
    return [(slot, b) for b in range(c0 // 512, (c0 + n - 1) // 512 + 1)]


def build_nc():
    nc = bass.Bass("TRN2", target_bir_lowering=False)
    dt = lambda name, shape, dtype=F32, kind="ExternalInput": nc.dram_tensor(name, shape, dtype, kind=kind).ap()
    x_d = dt("x", [NB, S, D])
    gains_d = dt("gains", [4, 128, D])
    wgu_d = [dt("wgu%d" % f, [NFF, 128, 2 * 8 * 128]) for f in range(2)]
    wd_d = [dt("wd%d" % f, [NFF, 128, D]) for f in range(2)]
    wfox_d = dt("wfox", [8, 128, 8 * 256])
    wmisc_d = dt("wmisc", [128, 8 * 40])
    wgla_d = dt("wgla", [6, 128, 8 * 256])
    fb_d = dt("fb", [8, 1])
    wg2a_d = dt("wg2a", [17, 256])
    gon_d = dt("gon", [128, 1])
    wout_d = dt("wout", [128, 8 * D])
    y_d = dt("y", [NB, S, D], kind="ExternalOutput")
    if DEBUG:
        dbg_d = dt("dbg", [8, 128, 16 * D], kind="ExternalOutput")

    with contextlib.ExitStack() as st:
        sb = lambda name, shape, dtype: st.enter_context(nc.sbuf_tensor(name, shape, dtype))
        x_sb = sb("x_sb", [128, TT, D], F32)
        hT = sb("hT", [128, 8, S], BF16)
        gt = sb("gt", [128, D], F32)
        hbf = sb("hbf", [128, 2, D], BF16)
        junk = sb("junk", [128, D], BF16)
        stats = sb("stats", [128, 64], F32)
        ident = sb("ident", [128, 128], BF16)
        ones_bf = sb("ones_bf", [128, 128], BF16)
        mask01 = sb("mask01", [128, 128], BF16)
        tri = sb("tri", [128, 128], BF16)
        ind = sb("ind", [128, 2], BF16)
        ones_f = sb("ones_f", [128, 64], F32)
        cst = sb("cst", [128, 4], F32)
        scan1 = sb("scan1", [8, 512], F32)
        fb = sb("fbt", [8, 1], F32)
        gon = sb("gont", [128, 1], F32)
        wg2a = sb("wg2at", [32, 256], BF16)
        wmisc = sb("wmisct", [128, 8, 40], BF16)
        pieces = sb("pieces", [72, S], BF16)
        fcar = sb("fcar", [8, 2], F32)
        decs = sb("decs", [128, TT * 4], F32)
        A = sb("slotA", [128, 24576], BF16)
        B = sb("slotB", [128, 14336], BF16)
        C = sb("slotC", [128, 8192], BF16)
        banks = [st.enter_context(nc.psum_tensor("bank%d" % i, [128, 512], F32)) for i in range(8)]
        PS = lambda b: ("ps", b)

        eps_ap = cst[:, 0:1]
        one_ap = cst[:, 1:2]

        P = Prog(nc)

        P.pool(MEMSET(ident[:], 0.0), writes=["ident"])
        P.pool(lambda e: e.affine_select(out=ident[:], in_=ident[:], pattern=[[-1, 128]], compare_op=ALU.not_equal,
                                         fill=1.0, base=0, channel_multiplier=1), writes=["ident"])
        P.pool(MEMSET(ones_bf[:], 1.0), writes=["ones_bf"])
        P.pool(MEMSET(ones_f[:], 1.0), writes=["ones_f"])
        P.pool(MEMSET(mask01[:], 1.0), writes=["mask01"])
        P.pool(lambda e: e.affine_select(out=mask01[:], in_=mask01[:], pattern=[[1, 128]], compare_op=ALU.is_ge,
                                         fill=0.0, base=0, channel_multiplier=-1), writes=["mask01"])
        P.pool(MEMSET(tri[:], 1.0 / 16.0), writes=["tri"])
        P.pool(lambda e: e.affine_select(out=tri[:], in_=tri[:], pattern=[[-1, 128]], compare_op=ALU.is_gt,
                                         fill=0.0, base=0, channel_multiplier=1), writes=["tri"])
        P.pool(MEMSET(tri[64:128, 0:64], 0.0), writes=["tri"])
        P.pool(MEMSET(ind[:], 0.0), writes=["ind"])
        P.pool(MEMSET(ind[0:64, 0:1], 1.0 / 16.0), writes=["ind"])
        P.pool(MEMSET(ind[64:128, 1:2], 1.0 / 16.0), writes=["ind"])
        P.pool(MEMSET(cst[:, 0:1], EPS), writes=["cst"])
        P.pool(MEMSET(cst[:, 1:2], 1.0), writes=["cst"])
        P.pool(MEMSET(scan1[:], 1.0), writes=["scan1"])
        P.pool(MEMSET(wg2a[:], 0.0), writes=["wg2a"])
        P.dma("sp", "c_fb", DMA(fb[:], fb_d[:, :]), writes=["fb"])
        P.dma("sp", "c_gon", DMA(gon[:], gon_d[:, :]), writes=["gon"])
        P.dma("pool", "c_wg2a", DMA(wg2a[0:17, :], wg2a_d[:, :]), reads=["wg2a"], writes=["wg2a"])
        P.dma("pool", "c_wmisc", DMA(wmisc[:].rearrange("p k c -> p (k c)"), wmisc_d[:, :]), writes=["wmisc"])

        dbg_n = [0]

        def dbg_dump(ap_src, ncols, reads):
            if not DEBUG:
                return
            i = dbg_n[0]
            dbg_n[0] += 1
            P.dma("sp", "dbg%d" % i, DMA(dbg_d[i, :, 0:ncols], ap_src), reads=reads, writes=[("dbg", i)])

        ssq = stats[:, 0:16]
        lnv = stats[:, 16:32]
        rstd = stats[:, 32:48]

        def norm_begin(gain_idx):
            P.dma("sp", "gain", DMA(gt[:], gains_d[gain_idx]), writes=["gt"])
            P.dve(MEMSET(ssq, 0.0), writes=[("ssq", g) for g in range(4)])

        def norm_stats(tg):
            for i in range(4 * tg, 4 * tg + 4):
                P.act(ACTF(junk[:], x_sb[:, i, :], AF.Square, accum_out=ssq[:, i:i + 1]),
                      reads=[("x", i, 0), ("x", i, 1)], writes=["junk", ("ssq", tg)])
            c = slice(4 * tg, 4 * tg + 4)
            P.act(ACTF(lnv[:, c], ssq[:, c], AF.Ln, bias=eps_ap, scale=1.0 / D), reads=[("ssq", tg), "cst"],
                  writes=[("lnv", tg)])
            P.act(ACTF(rstd[:, c], lnv[:, c], AF.Exp, scale=-0.5), reads=[("lnv", tg)], writes=[("rstd", tg)])

        def norm_apply(s, tg, final):
            for i in range(4 * tg, 4 * tg + 4):
                if final:
                    ob = C[:, 4096 + (i % 2) * 2048: 4096 + (i % 2 + 1) * 2048].bitcast(F32)
                    okeys = blk("C", 4096 + (i % 2) * 2048, 2048)
                    P.dve(STT(ob, x_sb[:, i, :], rstd[:, i:i + 1], gt[:], ALU.mult, ALU.mult),
                          reads=[("x", i, 0), ("x", i, 1), ("rstd", tg), "gt"], writes=okeys)
                    P.dma("sp", ("y", i % 2), DMA(y_d[s, i * 128:(i + 1) * 128, :], ob), reads=okeys,
                          writes=[("y", s, i)])
                    continue
                hb = hbf[:, i % 2, :]
                P.dve(STT(hb, x_sb[:, i, :], rstd[:, i:i + 1], gt[:], ALU.mult, ALU.mult),
                      reads=[("x", i, 0), ("x", i, 1), ("rstd", tg), "gt"], writes=[("hbf", i % 2)])
                bk = 6 + (i % 2)
                pT = banks[bk][:].bitcast(BF16).rearrange("p (k t) -> p k t", k=8)
                P.pe(TRS([(pT[:, k, :], hb[:, k * 128:(k + 1) * 128]) for k in range(8)], ident[:]),
                     reads=[("hbf", i % 2), "ident"], writes=[PS(bk)])
                P.act(ACOPY(hT[:, :, i * 128:(i + 1) * 128], pT), writes=[PS(bk), ("hT", i)])

        def make_norm_hook(s, final):
            def hook(i):
                if i % 4 != 3:
                    return
                tg = i // 4
                norm_stats(tg)
                if tg >= 1:
                    norm_apply(s, tg - 1, final)
                if tg == 3:
                    norm_apply(s, 3, final)
            return hook

        def norm_phase_load(s, gain_idx):
            norm_begin(gain_idx)
            hook = make_norm_hook(s, False)
            for i in range(TT):
                P.dma("pool" if s > 0 else "sp", ("x", i), DMA(x_sb[:, i, :], x_d[s, i * 128:(i + 1) * 128, :]),
                      writes=[("x", i, 0), ("x", i, 1)])
                hook(i)

        def ffn_phase(f, hook=None):
            wcnt = 0
            for gi, chunks in enumerate(GROUPS):
                abuf = gi % 2
                a0 = abuf * 12288
                for cl, c in enumerate(chunks):
                    wb = wcnt % 3
                    wcnt += 1
                    wv = C[:, wb * 2048:(wb + 1) * 2048]
                    wkeys = blk("C", wb * 2048, 2048)
                    P.dma("pool", ("wgu", wb), DMA(wv, wgu_d[f][c]), writes=wkeys)
                    w4 = wv.rearrange("p (g k f) -> p g k f", g=2, k=8)
                    dv = B[:, abuf * 6144 + cl * 1024: abuf * 6144 + (cl + 1) * 1024]
                    dkeys = blk("B", abuf * 6144 + cl * 1024, 1024)
                    P.dma("pool", ("wd", abuf, cl), DMA(dv, wd_d[f][c]), writes=dkeys)
                    for tg in range(4):
                        par = tg % 2
                        hk = [("hT", 4 * tg + j) for j in range(4)]
                        P.pe(MM(banks[par][:], [(w4[:, 0, k, :], hT[:, k, tg * 512:(tg + 1) * 512]) for k in range(8)]),
                             reads=wkeys + hk, writes=[PS(par)])
                        P.pe(MM(banks[2 + par][:], [(w4[:, 1, k, :], hT[:, k, tg * 512:(tg + 1) * 512]) for k in range(8)]),
                             reads=wkeys + hk, writes=[PS(2 + par)])
                        sgc = 6144 + par * 512
                        sg = C[:, sgc:sgc + 512]
                        P.act(ACTF(sg, banks[par][:], AF.Silu), writes=[PS(par)] + blk("C", sgc, 512))
                        ac = a0 + cl * 2048 + tg * 512
                        P.dve(TT_(A[:, ac:ac + 512], banks[2 + par][:], sg, ALU.mult),
                              reads=blk("C", sgc, 512), writes=[PS(2 + par)] + blk("A", ac, 512))
                n = len(chunks)
                for i in range(TT):
                    for dh in range(2):
                        bk = 4 + ((2 * i + dh) % 2)
                        pairs = []
                        rk = []
                        for cl in range(n):
                            ac = a0 + cl * 2048 + i * 128
                            dc = abuf * 6144 + cl * 1024 + dh * 512
                            pairs.append((A[:, ac:ac + 128], B[:, dc:dc + 512]))
                            rk += blk("A", ac, 128) + blk("B", dc, 512)
                        P.pe(MM(banks[bk][:], pairs), reads=rk, writes=[PS(bk)])
                        xv = x_sb[:, i, dh * 512:(dh + 1) * 512]
                        P.dve(STT(xv, banks[bk][:], 0.5, xv, ALU.mult, ALU.add), writes=[PS(bk), ("x", i, dh)])
                    if hook is not None and gi == len(GROUPS) - 1:
                        hook(i)

        def mix_phase(s, hook=None):
            concat0 = 0
            GK0 = 0
            GQ0 = 4096
            GV0 = 16384
            WO0 = 16384

            def load_wchunk(b, src, ncols):
                wv = C[:, b * 2048:b * 2048 + ncols]
                keys = blk("C", b * 2048, 2048)
                P.dma("pool", ("wchunk", b), DMA(wv, src), writes=keys)
                return wv.rearrange("p (k c) -> p k c", k=8), keys

            wc = [0]

            def next_wchunk(src, ncols=2048):
                b = wc[0] % 2
                wc[0] += 1
                return load_wchunk(b, src, ncols)

            for half in range(2):
                w3, wk = next_wchunk(wgla_d[4 + half])
                for hh in range(2):
                    h = 2 * half + hh
                    for tg in range(4):
                        bk = (hh * 4 + tg) % 2
                        hk = [("hT", 4 * tg + j) for j in range(4)]
                        P.pe(MM(banks[bk][:], [(w3[:, k, hh * 128:(hh + 1) * 128], hT[:, k, tg * 512:(tg + 1) * 512])
                                               for k in range(8)]), reads=wk + hk, writes=[PS(bk)])
                        oc = concat0 + (4 + h) * 2048 + tg * 512
                        P.act(ACTF(A[:, oc:oc + 512], banks[bk][:], AF.Silu), writes=[PS(bk)] + blk("A", oc, 512))
                        P.dve(TS(A[:, oc:oc + 512], A[:, oc:oc + 512], gon[:, 0:1], None, ALU.mult), reads=["gon"],
                              writes=blk("A", oc, 512))
            w3, wk = next_wchunk(wgla_d[0])
            for j in range(2):
                for tg in range(4):
                    bk = (j * 4 + tg) % 2
                    hk = [("hT", 4 * tg + jj) for jj in range(4)]
                    P.pe(MM(banks[bk][:], [(w3[:, k, j * 128:(j + 1) * 128], hT[:, k, tg * 512:(tg + 1) * 512])
                                           for k in range(8)]), reads=wk + hk, writes=[PS(bk)])
                    oc = GQ0 + j * 2048 + tg * 512
                    P.act(AMUL(A[:, oc:oc + 512], banks[bk][:], 0.125), writes=[PS(bk)] + blk("A", oc, 512))
            w3, wk = next_wchunk(wgla_d[1])
            for i in range(TT):
                bk = i % 2
                P.pe(MM(banks[bk][:, 0:256], [(hT[:, k, i * 128:(i + 1) * 128], w3[:, k, :]) for k in range(8)]),
                     reads=wk + [("hT", i)], writes=[PS(bk)])
                oc = GK0 + i * 256
                P.dve(COPY(A[:, oc:oc + 256], banks[bk][:, 0:256]), writes=[PS(bk)] + blk("A", oc, 256))
            for half in range(2):
                w3, wk = next_wchunk(wgla_d[2 + half])
                for i in range(TT):
                    bk = i % 2
                    P.pe(MM(banks[bk][:, 0:256], [(hT[:, k, i * 128:(i + 1) * 128], w3[:, k, :]) for k in range(8)]),
                         reads=wk + [("hT", i)], writes=[PS(bk)])
                    oc = GV0 + i * 512 + half * 256
                    P.act(ACOPY(A[:, oc:oc + 256], banks[bk][:, 0:256]), writes=[PS(bk)] + blk("A", oc, 256))

            if MIX_STOP == 1:
                return
            def Bf(c0, n):
                return B[:, 2 * c0:2 * (c0 + n)].bitcast(F32), blk("B", 2 * c0, 2 * n)

            def Bv(c0, n, dtype=BF16):
                v = B[:, c0:c0 + n]
                return (v.bitcast(F32) if dtype == F32 else v), blk("B", c0, n)

            def a_v(par): return Bv(par * 512, 512, F32)
            def lsg_v(par): return Bv(1024 + par * 256, 256)
            def eR_v(par): return Bv(1536 + par * 544, 544, F32)
            def S_v(par): return Bv(3072 + par * 1024, 1024, F32)
            def Sbf_v(par): return Bv(5120 + par * 512, 512)
            def osq_v(par): return Bv(6144 + par * 512, 512)
            def rs_v(par): return Bv(7168 if par == 0 else 10240, 1024, F32)
            def relu_v(par): return Bv(11264 + par * 512, 512, F32)
            t1_v = Bv(8192, 1024, F32)
            def glT_v(par): return Bv(9216 + par * 512, 512)
            def Cf(c0, n):
                return C[:, 4096 + 2 * c0: 4096 + 2 * (c0 + n)].bitcast(F32), blk("C", 4096 + 2 * c0, 2 * n)
            fz_v = Cf(0, 512)
            fa_v = Cf(512, 512)
            fc_v = Cf(1024, 512)
            fr_v = Cf(1536, 512)

            bmask = junk[:, 0:512]
            P.pool(MEMSET(bmask, 0.0), writes=["junk"])
            for j in range(2):
                P.pool(MEMSET(junk[0:64, j * 256:j * 256 + 128], 1.0), writes=["junk"])
                P.pool(MEMSET(junk[64:128, j * 256 + 128:j * 256 + 256], 1.0), writes=["junk"])
            Sinit, Sinit_k = S_v(1)
            P.dve(MEMSET(Sinit, 0.0), writes=Sinit_k)

            def tgroup_prologue(tg):
                par = tg % 2
                hk = [("hT", 4 * tg + j) for j in range(4)]
                P.pe(MM(banks[1][0:40, :], [(wmisc[:, k, :], hT[:, k, tg * 512:(tg + 1) * 512]) for k in range(8)]),
                     reads=["wmisc"] + hk, writes=[PS(1)])
                g, gk_ = glT_v(par)
                P.dve(MEMSET(g[0:32, :], 1.0), writes=gk_)
                P.dve(COPY(g[0:16, :], banks[1][0:16, :]), writes=[PS(1)] + gk_)
                fz, fzk = fz_v
                fa, fak = fa_v
                fc, fck = fc_v
                fr, frk = fr_v
                P.dve(TS(fz[0:8, :], banks[1][32:40, :], fb[0:8, 0:1], None, ALU.add), reads=["fb"],
                      writes=[PS(1)] + fzk)
                P.dve(STT(fa[0:8, :], fz[0:8, :], -1.0, fz[0:8, :], ALU.mult, ALU.max), reads=fzk, writes=fak)
                P.act(ACTF(fa[0:8, :], fa[0:8, :], AF.Exp, scale=-1.0), writes=fak)
                P.act(ACTF(fa[0:8, :], fa[0:8, :], AF.Ln, bias=one_ap[0:8, :]), reads=["cst"], writes=fak)
                P.dve(STT(fz[0:8, :], fz[0:8, :], 0.0, fa[0:8, :], ALU.min, ALU.subtract), reads=fak, writes=fzk)
                if tg == 0:
                    P.dve(lambda e: e.tensor_tensor_scan(out=fc[0:8, :], data0=scan1[:], data1=fz[0:8, :], initial=0.0,
                                                         op0=ALU.mult, op1=ALU.add),
                          reads=fzk + ["scan1"], writes=fck)
                else:
                    P.dve(lambda e: e.tensor_tensor_scan(out=fc[0:8, :], data0=scan1[:], data1=fz[0:8, :],
                                                         initial=fcar[0:8, 0:1], op0=ALU.mult, op1=ALU.add),
                          reads=fzk + ["scan1", "fcar"], writes=fck)
                P.dve(COPY(fcar[0:8, 0:1], fc[0:8, 511:512]), reads=fck, writes=["fcar"])
                tr = slice(tg * 512, (tg + 1) * 512)
                pk = [("pieces", tg)]
                P.dve(COPY(pieces[0:8, tr], fc[0:8, :]), reads=fck, writes=pk)
                P.dve(TT_(fr[0:8, :], fc[0:8, :], pieces[0:8, tr], ALU.subtract), reads=fck + pk, writes=frk)
                midt = C[:, 5120:5632]
                P.dve(COPY(midt[0:8, :], fr[0:8, :]), reads=frk, writes=fak)
                P.dve(COPY(pieces[32:40, tr], midt[0:8, :]), reads=fak, writes=pk)
                P.dve(TT_(fc[0:8, :], fr[0:8, :], midt[0:8, :], ALU.subtract), reads=frk + fak, writes=fck)
                P.dve(COPY(pieces[64:72, tr], fc[0:8, :]), reads=fck, writes=pk)

            def stage_A1(t):
                par = t % 2
                tg = t // 4
                g, gk_ = glT_v(tg % 2)
                c0 = (t % 4) * 128
                P.pe(MM(banks[0][:, 0:256], [(g[0:17, c0:c0 + 128], wg2a[0:17, :])]),
                     reads=gk_ + ["wg2a"], writes=[PS(0)])
                a, ak = a_v(par)
                r_, rk_ = relu_v(par)
                P.act(ACTF(a, banks[0][:, 0:256], AF.Abs), writes=[PS(0)] + ak)
                P.act(ACTF(r_, banks[0][:, 0:256], AF.Relu, scale=-1.0), writes=[PS(0)] + rk_)
                P.act(ACTF(a, a, AF.Exp, scale=-1.0), writes=ak)
                P.act(ACTF(a, a, AF.Ln, bias=one_ap), reads=["cst"], writes=ak)

            def stage_A1b(t):
                par = t % 2
                a, ak = a_v(par)
                r_, rk_ = relu_v(par)
                l, lk = lsg_v(par)
                P.dve(STT(l, r_, -1.0, a, ALU.mult, ALU.subtract), reads=ak + rk_, writes=lk)

            def stage_A2(t):
                par = t % 2
                l, lk = lsg_v(par)
                e_, ek = eR_v(par)
                P.pe(MMS([(banks[2][:, 0:256], [(tri[:], l)]),
                          (banks[2][:, 256:258], [(l[:, 0:128], ind[:])]),
                          (banks[2][:, 258:260], [(l[:, 128:256], ind[:])])]),
                     reads=lk + ["tri", "ind"], writes=[PS(2)])
                P.act(ACTF(e_[:, 0:260], banks[2][:, 0:260], AF.Exp), writes=[PS(2)] + ek)
                kc = GK0 + t * 256
                P.pool(TT_(A[:, kc:kc + 256], A[:, kc:kc + 256], e_[:, 0:256], ALU.mult), reads=ek, writes=blk("A", kc, 256))

            def stage_B1(t):
                kc = GK0 + t * 256
                vc = GV0 + t * 512
                e_, ek = eR_v(t % 2)
                for c in range(2):
                    par = c
                    pb_ = 3 + par
                    rows = slice(c * 64, (c + 1) * 64)
                    P.pe(MMS([(banks[pb_][:, j * 256:(j + 1) * 256],
                               [(A[rows, kc + j * 128: kc + (j + 1) * 128], A[rows, vc + j * 256: vc + (j + 1) * 256])])
                              for j in range(2)]),
                         reads=blk("A", kc, 256) + blk("A", vc, 512), writes=[PS(pb_)])
                for c in range(2):
                    par = c
                    pb_ = 3 + par
                    Sn, Snk = S_v(par)
                    Sp, Spk = S_v(1 - par)
                    for j in range(2):
                        P.dve(STT(Sn[:, j * 256:(j + 1) * 256], Sp[:, j * 256:(j + 1) * 256],
                                  e_[:, 256 + 2 * j + c: 256 + 2 * j + c + 1],
                                  banks[pb_][:, j * 256:(j + 1) * 256], ALU.mult, ALU.add),
                              reads=Spk + ek, writes=[PS(pb_)] + Snk)
                    sb_, sbk = Sbf_v(par)
                    P.dve(TT_(sb_, Sn, bmask, ALU.mult), reads=Snk + ["junk"], writes=sbk)

            def po_bank(t):
                return (5, 7, 6)[t % 3]

            def stage_B2(t):
                items = []
                rk = []
                bk = po_bank(t)
                for c in range(2):
                    sb_, sbk = Sbf_v(c)
                    rk += sbk
                    for h in range(4):
                        j, hh = h // 2, h % 2
                        lhsT = sb_[:, j * 256 + hh * 128: j * 256 + (hh + 1) * 128]
                        qc = GQ0 + j * 2048 + t * 128 + c * 64
                        rk += blk("A", qc, 64)
                        items.append((banks[bk][:, h * 128 + c * 64: h * 128 + (c + 1) * 64], [(lhsT, A[:, qc:qc + 64])]))
                P.pe(MMS(items), reads=rk, writes=[PS(bk)])

            def stage_B2b(t):
                bk = po_bank(t)
                osq, ok_ = osq_v(t % 2)
                P.act(ACTF(osq, banks[bk][:], AF.Square), writes=[PS(bk)] + ok_)

            def stage_C1(t):
                osq, ok_ = osq_v(t % 2)
                rs, rsk = rs_v(t % 2)
                P.pe(MM(banks[1][:], [(ones_bf[:], osq)]), reads=ok_ + ["ones_bf"], writes=[PS(1)])
                P.act(ACTF(rs, banks[1][:], AF.Ln, bias=eps_ap, scale=1.0 / 128.0), reads=["cst"], writes=[PS(1)] + rsk)
                P.act(ACTF(rs, rs, AF.Exp, scale=-0.5), writes=rsk)

            def stage_C2(t):
                bk = po_bank(t)
                rs, rsk = rs_v(t % 2)
                t1, t1k = t1_v
                P.dve(TT_(t1, banks[bk][:], rs, ALU.mult), reads=rsk, writes=[PS(bk)] + t1k)
                gv3 = A[:, concat0 + 4 * 2048: concat0 + 8 * 2048].rearrange("p (h t) -> p h t", h=4)[:, :, t * 128:(t + 1) * 128]
                gkeys = [("A", (concat0 + (4 + h) * 2048 + t * 128) // 512) for h in range(4)]
                P.pool(TT_(gv3, t1.rearrange("p (h t) -> p h t", h=4), gv3, ALU.mult), reads=t1k, writes=gkeys)

            fox_w = {}

            def fox_load(h):
                fox_w[h] = next_wchunk(wfox_d[h], 2048)

            fox_load(0)
            fox_load(1)
            en = (lambda n: True) if GLA_STAGES is None else (lambda n: n in GLA_STAGES)
            for step in range(TT + 6):
                t = step
                if t < TT and t % 4 == 0 and en("P"):
                    tgroup_prologue(t // 4)
                if t < TT and en("A1"):
                    stage_A1(t)
                if 0 <= step - 1 < TT and en("A2"):
                    stage_A2(step - 1)
                if 0 <= step - 3 < TT and en("B2"):
                    stage_B2(step - 3)
                if 0 <= step - 2 < TT and en("B1"):
                    stage_B1(step - 2)
                if t < TT and en("A1"):
                    stage_A1b(t)
                if 0 <= step - 3 < TT and en("B2"):
                    stage_B2b(step - 3)
                if 0 <= step - 4 < TT and en("C"):
                    stage_C1(step - 4)
                if 0 <= step - 5 < TT and en("C"):
                    stage_C2(step - 5)
            if MIX_STOP == 2:
                return
            for b in range(2):
                q0, k0 = b * 2048, 4096 + b * 2048
                P.pool(MEMSET(B[64:70, q0:q0 + 2048], -1.0),
                       writes=blk("B", q0, 2048) + [("auginit", b)] + [("augq", b, r) for r in range(3)])
                P.pool(MEMSET(B[64:70, k0:k0 + 2048], 1.0),
                       writes=blk("B", k0, 2048) + [("auginit", b)] + [("augk", b, r) for r in range(3)])
                v0 = 8192 + b * 3072
                P.dve(MEMSET(B[:, v0:v0 + 3072].rearrange("p (i c) -> p i c", c=192)[:, :, 64:128], 1.0),
                      writes=blk("B", v0, 3072))

            def proj_qk(h, tg):
                b = h % 2
                q0, k0 = b * 2048, 4096 + b * 2048
                w3, wk = fox_w[h]
                hk = [("hT", 4 * tg + j) for j in range(4)]
                P.pe(MM(banks[6][:], [(w3[:, k, 0:128], hT[:, k, tg * 512:(tg + 1) * 512]) for k in range(8)]),
                     reads=wk + hk, writes=[PS(6)])
                qc = q0 + tg * 512
                kc = k0 + tg * 512
                P.dve(TS(B[0:64, qc:qc + 512], banks[6][0:64, :], 0.125, None, ALU.mult), writes=[PS(6)] + blk("B", qc, 512))
                P.dve(COPY(B[0:64, kc:kc + 512], banks[6][64:128, :]), writes=[PS(6)] + blk("B", kc, 512))

            def proj_v(pair, tg):
                h = 2 * pair
                w3, wk = fox_w[h]
                v0 = 8192 + (pair % 2) * 3072
                v3 = B[:, v0:v0 + 3072].rearrange("p (i c) -> p i c", c=192)
                hk = [("hT", 4 * tg + j) for j in range(4)]
                P.pe(MMS([(banks[7][:, j * 128:(j + 1) * 128],
                           [(hT[:, k, (4 * tg + j) * 128:(4 * tg + j + 1) * 128], w3[:, k, 128:256]) for k in range(8)])
                          for j in range(4)]), reads=wk + hk, writes=[PS(7)])
                p4 = banks[7][:].rearrange("p (j e c) -> p j e c", j=4, e=2)
                vk = blk("B", v0 + 4 * tg * 192, 4 * 192)
                P.dve(COPY(v3[:, 4 * tg:4 * tg + 4, 0:64], p4[:, :, 0, :]), writes=[PS(7)] + vk)
                P.dve(COPY(v3[:, 4 * tg:4 * tg + 4, 128:192], p4[:, :, 1, :]), writes=[PS(7)] + vk)

            def proj_aug(h):
                b = h % 2
                q0, k0 = b * 2048, 4096 + b * 2048
                for r in range(3):
                    P.dma("sp", ("augq", b, r), DMA(B[67 + r:68 + r, q0:q0 + 2048], pieces[32 * r + h:32 * r + h + 1, :]),
                          reads=[("pieces", g_) for g_ in range(4)], writes=[("augq", b, r)])
                    P.dma("sp", ("augk", b, r), DMA(B[64 + r:65 + r, k0:k0 + 2048], pieces[32 * r + h:32 * r + h + 1, :]),
                          reads=[("pieces", g_) for g_ in range(4)], writes=[("augk", b, r)])

            SKEW = 3
            units = []
            for h in range(8):
                for tg in range(4):
                    nj = 4 * tg + 4
                    for j in range(nj):
                        units.append((h, tg, j, nj))
            grp_cnt = {}

            def unit_bufs(idx):
                r = idx % 4
                pr = (0, 1, 2, 5)[r]
                pT0 = 6144 + r * 512
                return pr, pT0

            def emit_qk(idx):
                h, tg, j, nj = units[idx]
                b = h % 2
                q0, k0 = b * 2048, 4096 + b * 2048
                augk = [("augq", b, r) for r in range(3)] + [("augk", b, r) for r in range(3)] + [("auginit", b)]
                t0 = tg * 512
                jj = j - 4 * tg
                off = jj * 128 if jj > 0 else 0
                W = 512 - off
                pr, pT0 = unit_bufs(idx)
                pTk = blk("C", pT0, 512)
                P.pe(MM(banks[pr][:, 0:W], [(B[0:70, k0 + j * 128: k0 + (j + 1) * 128],
                                             B[0:70, q0 + t0 + off: q0 + t0 + 512])]),
                     reads=blk("B", k0 + j * 128, 128) + blk("B", q0 + t0, 512) + augk, writes=[PS(pr)])
                P.act(ACTF(C[:, pT0:pT0 + W], banks[pr][:, 0:W], AF.Exp), writes=[PS(pr)] + pTk)
                if jj >= 0:
                    P.pool(TT_(C[:, pT0:pT0 + 128], C[:, pT0:pT0 + 128], mask01[:], ALU.mult),
                           reads=["mask01"], writes=pTk)

            def emit_pv(idx):
                h, tg, j, nj = units[idx]
                pair = h // 2
                odd = h % 2
                v0 = 8192 + (pair % 2) * 3072
                v3 = B[:, v0:v0 + 3072].rearrange("p (i c) -> p i c", c=192)
                g = (h, tg)
                if g not in grp_cnt:
                    grp_cnt[g] = len(grp_cnt)
                ob = 3 + (grp_cnt[g] % 2)
                jj = j - 4 * tg
                off = jj * 128 if jj > 0 else 0
                W = 512 - off
                pr, pT0 = unit_bufs(idx)
                pTk = blk("C", pT0, 512)
                lhsT = v3[:, j, 64:192] if odd else v3[:, j, 0:128]
                P.pe(MM(banks[ob][:, off:512], [(lhsT, C[:, pT0:pT0 + W])], start=(j == 0), stop=(j == nj - 1)),
                     reads=pTk + blk("B", v0 + j * 192, 192), writes=[PS(ob)])
                if j == nj - 1:
                    rc0 = 4096 + (grp_cnt[g] % 2) * 1024
                    rcp = C[:, rc0:rc0 + 1024].bitcast(F32)
                    rck = blk("C", rc0, 1024)
                    drow = slice(64, 128) if odd else slice(0, 64)
                    srow = slice(0, 64) if odd else slice(64, 128)
                    P.dve(lambda e, o_=rcp[drow, :], i_=banks[ob][srow, :]: e.reciprocal(out=o_, in_=i_),
                          writes=[PS(ob)] + rck)
                    oc = concat0 + (h // 2) * 2048 + tg * 512
                    P.dve(TT_(A[drow, oc:oc + 512], banks[ob][drow, :], rcp[drow, :], ALU.mult), reads=rck,
                          writes=[PS(ob)] + blk("A", oc, 512))

            if MIX_STOP == 3:
                return
            for tg in range(4):
                proj_qk(0, tg)
                proj_v(0, tg)
            proj_aug(0)
            fox_load(2)
            wo3 = A[:, WO0:WO0 + 8192].rearrange("p (k d) -> p k d", k=8)
            wok = blk("A", WO0, 8192)
            P.dma("pool", "wout", DMA(A[:, WO0:WO0 + 8192], wout_d[:, :]), writes=wok)
            if MIX_STOP == 4:
                return
            side = {}
            base = 0
            for h in range(8):
                if h + 1 < 8:
                    for tg in range(4):
                        side.setdefault(base + 4 + 8 * tg, []).append(("qk", h + 1, tg))
                    side.setdefault(base + 5, []).append(("aug", h + 1))
                    if h % 2 == 1:
                        for tg in range(4):
                            side.setdefault(base + 8 + 8 * tg, []).append(("v", (h + 1) // 2, tg))
                    if h + 3 < 8:
                        side.setdefault(base + 34, []).append(("load", h + 3))
                base += 40
            for idx in range(len(units) + SKEW):
                if idx < len(units):
                    emit_qk(idx)
                if idx - SKEW >= 0:
                    emit_pv(idx - SKEW)
                for item in side.get(idx, ()):
                    if item[0] == "qk":
                        proj_qk(item[1], item[2])
                    elif item[0] == "v":
                        proj_v(item[1], item[2])
                    elif item[0] == "aug":
                        proj_aug(item[1])
                    else:
                        fox_load(item[1])

            if MIX_STOP == 5:
                return
            cT = A[:, concat0:concat0 + 16384].rearrange("p (k t) -> p k t", k=8)
            for i in range(TT):
                for dh in range(2):
                    bk = (2 * i + dh) % 2
                    rk = list(wok)
                    for k in range(8):
                        rk += blk("A", concat0 + k * 2048 + i * 128, 128)
                    P.pe(MM(banks[bk][:], [(cT[:, k, i * 128:(i + 1) * 128], wo3[:, k, dh * 512:(dh + 1) * 512])
                                           for k in range(8)]), reads=rk, writes=[PS(bk)])
                    xv = x_sb[:, i, dh * 512:(dh + 1) * 512]
                    P.dve(TT_(xv, banks[bk][:], xv, ALU.add), writes=[PS(bk), ("x", i, dh)])
                if hook is not None:
                    hook(i)

        def dump_x(s):
            for i in range(TT):
                P.dma("sp", ("yd", i % 2), DMA(y_d[s, i * 128:(i + 1) * 128, :], x_sb[:, i, :]),
                      reads=[("x", i, 0), ("x", i, 1)], writes=[("y", s, i)])

        for s in range(NB if STOP_AFTER is None else 1):
            norm_phase_load(s, 0)
            if STOP_AFTER == 0:
                dump_x(s); continue
            if STOP_AFTER == 1:
                ffn_phase(0)
                dump_x(s); continue
            norm_begin(1)
            ffn_phase(0, make_norm_hook(s, False))
            if STOP_AFTER == 2:
                mix_phase(s)
                dump_x(s); continue
            norm_begin(2)
            mix_phase(s, make_norm_hook(s, False))
            if STOP_AFTER == 3:
                ffn_phase(1)
                dump_x(s); continue
            norm_begin(3)
            ffn_phase(1, make_norm_hook(s, True))
        P.op("sp", lambda e: None, reads=[("y", s, i) for s in range(NB if STOP_AFTER is None else 1) for i in range(TT)]
             + ([("dbg", i) for i in range(dbg_n[0])] if DEBUG else []))
        P.finalize(st)
    return nc


def _prep_weights(inp):
    f32 = lambda a: np.ascontiguousarray(np.asarray(a, dtype=np.float32))
    out = {}
    gains = np.stack([np.asarray(inp["ffn1_norm"])[0], np.asarray(inp["mix_norm"])[0],
                      np.asarray(inp["ffn2_norm"])[0], np.asarray(inp["final_norm"])], axis=0)
    out["gains"] = f32(np.broadcast_to(gains[:, None, :], (4, 128, D)))
    for f, pre in enumerate(("ffn1", "ffn2")):
        wg = np.asarray(inp[pre + "_w_gate"])[0]
        wu = np.asarray(inp[pre + "_w_up"])[0]
        wd = np.asarray(inp[pre + "_w_down"])[0]
        g = wg.reshape(8, 128, NFF, 128).transpose(2, 1, 0, 3)
        u = wu.reshape(8, 128, NFF, 128).transpose(2, 1, 0, 3)
        out["wgu%d" % f] = f32(np.stack([g, u], axis=2).reshape(NFF, 128, 2 * 8 * 128))
        out["wd%d" % f] = f32(wd.reshape(NFF, 128, D))
    win = np.asarray(inp["w_in"])[0]
    w3 = win.reshape(8, 128, 3096).transpose(1, 0, 2)
    fq, fk, fv = w3[:, :, 0:512], w3[:, :, 512:1024], w3[:, :, 1024:1536]
    ff = w3[:, :, 1536:1544]
    gq, gk, gv = w3[:, :, 1544:1800], w3[:, :, 1800:2056], w3[:, :, 2056:2568]
    glow, gout = w3[:, :, 2568:2584], w3[:, :, 2584:3096]
    wfox = np.stack([np.concatenate([fq[:, :, h * 64:(h + 1) * 64], fk[:, :, h * 64:(h + 1) * 64],
                                     fv[:, :, (h // 2) * 128:(h // 2) * 128 + 128]], axis=2) for h in range(8)], axis=0)
    out["wfox"] = f32(wfox.reshape(8, 128, 8 * 256))
    misc = np.zeros((128, 8, 40), np.float32)
    misc[:, :, 0:16] = glow
    misc[:, :, 32:40] = ff
    out["wmisc"] = f32(misc.reshape(128, 8 * 40))
    wgla = np.stack([gq, gk, gv[:, :, 0:256], gv[:, :, 256:512], gout[:, :, 0:256], gout[:, :, 256:512]], axis=0)
    out["wgla"] = f32(wgla.reshape(6, 128, 8 * 256))
    out["fb"] = f32(np.asarray(inp["fox_forget_bias"])[0].reshape(8, 1))
    out["wg2a"] = f32(np.concatenate([np.asarray(inp["gla_w_gate_up"])[0],
                                      np.asarray(inp["gla_gate_bias"])[0][None, :]], axis=0))
    out["gon"] = f32(np.asarray(inp["gla_out_norm"])[0].reshape(128, 1))
    wout = np.asarray(inp["w_out"])[0]
    out["wout"] = f32(wout.reshape(8, 128, D).transpose(1, 0, 2).reshape(128, 8 * D))
    return out


_NC_CACHE = {}


def kernel(**inputs):
    x = np.asarray(inputs["x"], dtype=np.float32)
    w = _prep_weights(inputs)
    if "nc" not in _NC_CACHE:
        _NC_CACHE["nc"] = build_nc()
    nc = _NC_CACHE["nc"]
    in_maps = []
    for c in range(N_CORES):
        m = dict(w)
        m["x"] = np.ascontiguousarray(x[c * NB:(c + 1) * NB])
        in_maps.append(m)
    res = run_bass_kernel_spmd(nc, in_maps, core_ids=list(range(N_CORES)))
    out = np.concatenate([np.asarray(r["y"]) for r in res.results], axis=0)
    if DEBUG:
        kernel.dbg = [np.asarray(r["dbg"]) for r in res.results]
    return out.astype(np.float32)
```
